# Optimizing a Trainium2 kernel written in Bass

```python
import numpy as np
import jax
import jax.numpy as jnp
from jax import lax

D_MODEL = 4096
BATCH = 1
SEQ = 8192
DEPTH = 1

HEAD_DIM = 128
N_HEADS = 16
N_KV_HEADS = 4
GROUP = N_HEADS // N_KV_HEADS
CMP_BLOCK = 32
CMP_STRIDE = 16
CMP_HIDDEN = 512
SEL_BLOCK = 64
N_SELECT = 16
WINDOW = 512
Q_BLOCK = 128
CONV_CH = 2048
CONV_WIDTH = 31
D_FF = 11008

NEG_INF = -1e30
FORCE = 1e9
EPS = 1e-6

Q_W = N_HEADS * HEAD_DIM
KV_W = N_KV_HEADS * HEAD_DIM
NSA_GATE_W = N_HEADS * 3
CONV_IN_W = 2 * CONV_CH
MERGE_W = 2 * D_MODEL
SPLITS = (Q_W, KV_W, KV_W, KV_W, KV_W, KV_W, KV_W, NSA_GATE_W, CONV_IN_W, MERGE_W)
N_IN = Q_W + 6 * KV_W + NSA_GATE_W + CONV_IN_W + MERGE_W

kernel_name = 'hybrid_nsa_conformer_macaron'


def rms_norm(x, g):
    xf = x.astype(jnp.float32)
    y = xf * lax.rsqrt(jnp.mean(xf * xf, axis=-1, keepdims=True) + EPS)
    return (y * g.astype(jnp.float32)).astype(x.dtype)


def layer_norm(x, g, b):
    xf = x.astype(jnp.float32)
    mu = jnp.mean(xf, axis=-1, keepdims=True)
    var = jnp.mean(jnp.square(xf - mu), axis=-1, keepdims=True)
    y = (xf - mu) * lax.rsqrt(var + EPS)
    return (y * g.astype(jnp.float32) + b.astype(jnp.float32)).astype(x.dtype)


def swiglu(h, w_gate, w_up, w_down):
    return (jax.nn.silu(h @ w_gate) * (h @ w_up)) @ w_down


def masked_softmax(s, mask):
    p = jax.nn.softmax(jnp.where(mask, s.astype(jnp.float32), NEG_INF), axis=-1)
    return jnp.where(mask, p, 0.0)


def split_heads(a):
    B, T, _ = a.shape
    return a.reshape(B, T, N_KV_HEADS, HEAD_DIM).transpose(0, 2, 1, 3)


def compress(kv, pos, w1, w2):
    B, H, T, DH = kv.shape
    n_cmp = (T - CMP_BLOCK) // CMP_STRIDE + 1
    idx = jnp.arange(n_cmp)[:, None] * CMP_STRIDE + jnp.arange(CMP_BLOCK)[None, :]
    blk = kv[:, :, idx] + pos
    flat = blk.reshape(B, H, n_cmp, CMP_BLOCK * DH)
    return jax.nn.gelu(flat @ w1) @ w2


def nsa_attention(q, k_cmp, v_cmp, k_slc, v_slc, k_win, v_win, gates):
    B, HKV, G, T, DH = q.shape
    scale = HEAD_DIM ** -0.5
    n_cmp = k_cmp.shape[2]
    n_slc = T // SEL_BLOCK
    n_top = min(N_SELECT, n_slc)
    n_blocks = T // Q_BLOCK
    cmp_end = jnp.arange(n_cmp) * CMP_STRIDE + CMP_BLOCK - 1
    cj = jnp.arange(n_cmp)[:, None] * CMP_STRIDE
    si = jnp.arange(n_slc)[None, :] * SEL_BLOCK
    overlap = jnp.clip(jnp.minimum(cj + CMP_BLOCK, si + SEL_BLOCK) - jnp.maximum(cj, si), 0, None)
    cmp_to_slc = (overlap / CMP_BLOCK).astype(jnp.float32)
    slc_ids = jnp.arange(n_slc)
    kb = k_slc.reshape(B, HKV, n_slc, SEL_BLOCK, DH)
    vb = v_slc.reshape(B, HKV, n_slc, SEL_BLOCK, DH)
    pad = ((0, 0), (0, 0), (WINDOW, 0), (0, 0))
    kw = jnp.pad(k_win, pad)
    vw = jnp.pad(v_win, pad)
    b_ix = jnp.arange(B)[:, None, None, None]
    h_ix = jnp.arange(HKV)[None, :, None, None]

    def block(i):
        start = i * Q_BLOCK
        t = start + jnp.arange(Q_BLOCK)
        qb = lax.dynamic_slice_in_dim(q, start, Q_BLOCK, axis=3)
        gb = lax.dynamic_slice_in_dim(gates, start, Q_BLOCK, axis=3)
        m_cmp = cmp_end[None, :] <= t[:, None]
        s = jnp.einsum('bhgqd,bhjd->bhgqj', qb, k_cmp) * scale
        p_cmp = masked_softmax(s, m_cmp)
        o_cmp = jnp.einsum('bhgqj,bhjd->bhgqd', p_cmp.astype(v_cmp.dtype), v_cmp)
        imp = jnp.einsum('bhgqj,js->bhqs', p_cmp, cmp_to_slc)
        cur = t // SEL_BLOCK
        forced = (slc_ids[None, :] == 0) | (slc_ids[None, :] == cur[:, None]) | (slc_ids[None, :] == cur[:, None] - 1)
        valid = slc_ids[None, :] * SEL_BLOCK <= t[:, None]
        score = jnp.where(forced, FORCE, jnp.where(valid, imp, NEG_INF))
        _, idx = lax.top_k(score, n_top)
        ks = kb[b_ix, h_ix, idx].reshape(B, HKV, Q_BLOCK, n_top * SEL_BLOCK, DH)
        vs = vb[b_ix, h_ix, idx].reshape(B, HKV, Q_BLOCK, n_top * SEL_BLOCK, DH)
        kpos = (idx[..., None] * SEL_BLOCK + jnp.arange(SEL_BLOCK)).reshape(B, HKV, Q_BLOCK, n_top * SEL_BLOCK)
        m_slc = (kpos <= t[:, None])[:, :, None]
        s = jnp.einsum('bhgqd,bhqkd->bhgqk', qb, ks) * scale
        p = masked_softmax(s, m_slc)
        o_slc = jnp.einsum('bhgqk,bhqkd->bhgqd', p.astype(vs.dtype), vs)
        kwb = lax.dynamic_slice_in_dim(kw, start, WINDOW + Q_BLOCK, axis=2)
        vwb = lax.dynamic_slice_in_dim(vw, start, WINDOW + Q_BLOCK, axis=2)
        wpos = start - WINDOW + jnp.arange(WINDOW + Q_BLOCK)
        m_win = (wpos[None, :] >= 0) & (wpos[None, :] <= t[:, None]) & (t[:, None] - wpos[None, :] < WINDOW)
        s = jnp.einsum('bhgqd,bhkd->bhgqk', qb, kwb) * scale
        p = masked_softmax(s, m_win)
        o_win = jnp.einsum('bhgqk,bhkd->bhgqd', p.astype(vwb.dtype), vwb)
        return gb[..., 0:1] * o_cmp + gb[..., 1:2] * o_slc + gb[..., 2:3] * o_win

    out = lax.map(block, jnp.arange(n_blocks))
    return out.transpose(1, 0, 4, 2, 3, 5).reshape(B, T, HKV * G * DH)


def conv_module(u_glu, conv_w, conv_b, ln_g, ln_b, w_o):
    a, g = jnp.split(u_glu, 2, axis=-1)
    u = a * jax.nn.sigmoid(g)
    u = lax.conv_general_dilated(u, conv_w[:, None, :], (1,), [(CONV_WIDTH - 1, 0)],
                                 dimension_numbers=('NWC', 'WIO', 'NWC'),
                                 feature_group_count=CONV_CH) + conv_b
    u = jax.nn.silu(layer_norm(u, ln_g, ln_b))
    return u @ w_o


def setup_inputs(seed: int = 0) -> dict:
    key = jax.random.key(seed)
    ks = iter(jax.random.split(key, 32))
    L = DEPTH

    def w(shape, fan_in):
        return jax.random.normal(next(ks), shape, jnp.float32) * fan_in ** -0.5

    def gain(shape):
        return 1.0 + 0.05 * jax.random.normal(next(ks), shape, jnp.float32)

    def small(shape, s):
        return s * jax.random.normal(next(ks), shape, jnp.float32)

    return {
        'x': jax.random.normal(next(ks), (BATCH, SEQ, D_MODEL), jnp.float32),
        'ffn1_norm': gain((L, D_MODEL)),
        'ffn1_w_gate': w((L, D_MODEL, D_FF), D_MODEL),
        'ffn1_w_up': w((L, D_MODEL, D_FF), D_MODEL),
        'ffn1_w_down': w((L, D_FF, D_MODEL), D_FF),
        'mix_norm': gain((L, D_MODEL)),
        'w_in': w((L, D_MODEL, N_IN), D_MODEL),
        'q_norm': gain((L, HEAD_DIM)),
        'k_norm': gain((L, 3, HEAD_DIM)),
        'cmp_pos_k': small((L, CMP_BLOCK, HEAD_DIM), 0.1),
        'cmp_k_w1': w((L, CMP_BLOCK * HEAD_DIM, CMP_HIDDEN), CMP_BLOCK * HEAD_DIM),
        'cmp_k_w2': w((L, CMP_HIDDEN, HEAD_DIM), CMP_HIDDEN),
        'cmp_pos_v': small((L, CMP_BLOCK, HEAD_DIM), 0.1),
        'cmp_v_w1': w((L, CMP_BLOCK * HEAD_DIM, CMP_HIDDEN), CMP_BLOCK * HEAD_DIM),
        'cmp_v_w2': w((L, CMP_HIDDEN, HEAD_DIM), CMP_HIDDEN),
        'nsa_w_o': w((L, Q_W, D_MODEL), Q_W),
        'conv_w': w((L, CONV_WIDTH, CONV_CH), CONV_WIDTH),
        'conv_b': small((L, CONV_CH), 0.02),
        'conv_ln_g': gain((L, CONV_CH)),
        'conv_ln_b': small((L, CONV_CH), 0.02),
        'conv_w_o': w((L, CONV_CH, D_MODEL), CONV_CH),
        'w_out': w((L, D_MODEL, D_MODEL), D_MODEL),
        'ffn2_norm': gain((L, D_MODEL)),
        'ffn2_w_gate': w((L, D_MODEL, D_FF), D_MODEL),
        'ffn2_w_up': w((L, D_MODEL, D_FF), D_MODEL),
        'ffn2_w_down': w((L, D_FF, D_MODEL), D_FF),
    }


def reference(x, ffn1_norm, ffn1_w_gate, ffn1_w_up, ffn1_w_down, mix_norm, w_in, q_norm, k_norm,
              cmp_pos_k, cmp_k_w1, cmp_k_w2, cmp_pos_v, cmp_v_w1, cmp_v_w2, nsa_w_o,
              conv_w, conv_b, conv_ln_g, conv_ln_b, conv_w_o, w_out,
              ffn2_norm, ffn2_w_gate, ffn2_w_up, ffn2_w_down):
    B, T, _ = x.shape
    offs = [int(o) for o in np.cumsum(SPLITS)[:-1]]
    for l in range(DEPTH):
        x = x + 0.5 * swiglu(rms_norm(x, ffn1_norm[l]), ffn1_w_gate[l], ffn1_w_up[l], ffn1_w_down[l])
        h = rms_norm(x, mix_norm[l])
        proj = h @ w_in[l]
        q, k_c, v_c, k_s, v_s, k_w, v_w, g_nsa, u_glu, g_mrg = jnp.split(proj, offs, axis=-1)
        q = rms_norm(q.reshape(B, T, N_HEADS, HEAD_DIM), q_norm[l])
        q = q.reshape(B, T, N_KV_HEADS, GROUP, HEAD_DIM).transpose(0, 2, 3, 1, 4)
        k_cmp = rms_norm(compress(split_heads(k_c), cmp_pos_k[l], cmp_k_w1[l], cmp_k_w2[l]), k_norm[l, 0])
        v_cmp = compress(split_heads(v_c), cmp_pos_v[l], cmp_v_w1[l], cmp_v_w2[l])
        k_slc = rms_norm(split_heads(k_s), k_norm[l, 1])
        k_win = rms_norm(split_heads(k_w), k_norm[l, 2])
        gates = jax.nn.sigmoid(g_nsa).reshape(B, T, N_KV_HEADS, GROUP, 3).transpose(0, 2, 3, 1, 4)
        y_nsa = nsa_attention(q, k_cmp, v_cmp, k_slc, split_heads(v_s), k_win, split_heads(v_w), gates) @ nsa_w_o[l]
        y_conv = conv_module(u_glu, conv_w[l], conv_b[l], conv_ln_g[l], conv_ln_b[l], conv_w_o[l])
        g = jax.nn.sigmoid(g_mrg).reshape(B, T, 2, D_MODEL)
        x = x + (g[:, :, 0] * y_nsa + g[:, :, 1] * y_conv) @ w_out[l]
        x = x + 0.5 * swiglu(rms_norm(x, ffn2_norm[l]), ffn2_w_gate[l], ffn2_w_up[l], ffn2_w_down[l])
    return x
```

```python
import numpy as np
from contextlib import ExitStack
import concourse.bass as bass
import concourse.mybir as mybir
from concourse.bass_utils import run_bass_kernel_spmd

F32 = mybir.dt.float32
BF16 = mybir.dt.bfloat16
AF = mybir.ActivationFunctionType
ALU = mybir.AluOpType
AX = mybir.AxisListType

ENGS = ("pe", "act", "dve", "pool", "sp")
EPS = 1e-6
TOK = 1024
CHK = 512
NCORE = 8
SEQ = 8192
NEGB = -8192.0
SCALE = 128.0 ** -0.5
MODE = "split"
DEBUG = False
DBG = {}

CFG_FULL = dict(D=4096, FF=11008, KG=22, HALVES=[(0, 44), (44, 42)], FHMAX=44, CH=2048)


class Slot:
    def __init__(self, name, buf):
        self.name = name
        self.buf = buf
        self.cnt = 0
        self.free = []
        self.tok = None


class Prog:
    def __init__(self, nc, es):
        self.nc = nc
        self.es = es
        self.q = {e: [] for e in ENGS}
        self.cnt = {e: 0 for e in ENGS}
        self.waited = {e: {} for e in ENGS}
        self.slots = []
        self.nslot = 0
        self.esems = {e: es.enter_context(nc.semaphore("sem_" + e)) for e in ENGS}
        self.ssems = {}
        self.lastw = {}
        self.readers = {}
        self.nops = 0
        self.rings = {}
        self.ring_i = {}

    def slot(self, name, buf=None):
        self.nslot += 1
        s = Slot(f"{name}_{self.nslot}", buf)
        self.slots.append(s)
        return s

    def _waits(self, eng, deps):
        for d in deps:
            if d is None:
                continue
            k, v = d
            if self.waited[eng].get(k, 0) >= v:
                continue
            self.waited[eng][k] = v
            self.q[eng].append(("wait", k, v))

    def op(self, eng, fn, deps=(), sig=False):
        self._waits(eng, deps)
        tok = None
        if sig:
            self.cnt[eng] += 1
            tok = (eng, self.cnt[eng])
        self.q[eng].append(("op", fn, tok))
        self.nops += 1
        return tok

    def dma(self, eng, slot, fn, deps=()):
        self._waits(eng, deps)
        slot.cnt += 16
        tok = ("slot:" + slot.name, slot.cnt)
        self.q[eng].append(("dma", fn, slot))
        self.nops += 1
        return tok

    def _auto_deps(self, eng, reads, writes):
        deps = []
        for k in list(reads) + list(writes):
            w = self.lastw.get(k)
            if w is not None and not (eng == "pe" and w[0] == "pe"):
                deps.append(w)
        for k in writes:
            for r in self.readers.get(k, {}).values():
                if not (eng == "pe" and r[0] == "pe"):
                    deps.append(r)
        return deps

    def _auto_post(self, tok, reads, writes):
        for k in writes:
            self.lastw[k] = tok
            self.readers[k] = {}
        for k in reads:
            self.readers.setdefault(k, {})[tok[0]] = tok

    def aop(self, eng, fn, reads=(), writes=(), extra=()):
        deps = self._auto_deps(eng, reads, writes) + list(extra)
        tok = self.op(eng, fn, deps, sig=True)
        self._auto_post(tok, reads, writes)
        return tok

    def adma(self, eng, fn, reads=(), writes=(), extra=(), name="d"):
        ring = self.rings.setdefault(eng, [])
        nmax = 24 if eng == "sp" else 8
        i = self.ring_i.get(eng, 0)
        self.ring_i[eng] = i + 1
        if len(ring) < nmax:
            ring.append(self.slot("ring" + eng))
        s = ring[i % nmax]
        deps = self._auto_deps(eng, reads, writes) + list(extra)
        if s.cnt > 0:
            deps.append(("slot:" + s.name, s.cnt))
        tok = self.dma(eng, s, fn, deps)
        self._auto_post(tok, reads, writes)
        return tok

    def barrier(self):
        toks = [(e, self.cnt[e]) for e in ENGS if self.cnt[e] > 0]
        toks += [("slot:" + s.name, s.cnt) for s in self.slots if s.cnt > 0]
        for eng in ENGS:
            self._waits(eng, toks)
        self.lastw = {}
        self.readers = {}

    def flush(self, final_waits=None):
        nc = self.nc
        handles = {"pe": "tensor", "act": "scalar", "dve": "vector", "pool": "gpsimd", "sp": "sync"}
        final_waits = final_waits or {}

        def semof(k):
            if k.startswith("slot:"):
                n = k[5:]
                if n not in self.ssems:
                    self.ssems[n] = self.es.enter_context(nc.semaphore("ss_" + n))
                return self.ssems[n]
            return self.esems[k]

        def run(eng, e):
            for item in self.q[eng]:
                if item[0] == "wait":
                    e.wait_ge(semof(item[1]), item[2])
                elif item[0] == "op":
                    ins = item[1](e)
                    if item[2] is not None:
                        ins.then_inc(self.esems[eng], 1)
                else:
                    ins = item[1](e)
                    if item[2].name.startswith("cc"):
                        ins.then_inc(semof("slot:" + item[2].name))
                    else:
                        ins.then_inc(semof("slot:" + item[2].name), 16)
            for (k, v) in final_waits.get(eng, []):
                e.wait_ge(semof(k), v)

        with nc.Block() as block:
            for eng in ENGS:
                getattr(block, handles[eng])(lambda e, eng=eng: run(eng, e))
        self.q = {e: [] for e in ENGS}


class Ctx:
    pass


def MM(P, out, lhsT, rhs, start, stop, R, W):
    return P.aop("pe", lambda e: e.matmul(out, lhsT=lhsT, rhs=rhs, start=start, stop=stop), R, W)


def TR(P, out, in_, ident, R, W):
    return P.aop("pe", lambda e: e.transpose(out=out, in_=in_, identity=ident), R, W)


def ACT(P, out, in_, func, R, W, scale=None, bias=None, accum=None):
    kw = {}
    if scale is not None:
        kw["scale"] = scale
    if bias is not None:
        kw["bias"] = bias
    if accum is not None:
        kw["accum_out"] = accum
    return P.aop("act", lambda e: e.activation(out=out, in_=in_, func=func, **kw), R, W)


def TS(P, eng, out, in0, s1, s2, op0, op1, R, W):
    if op1 is None:
        return P.aop(eng, lambda e: e.tensor_scalar(out=out, in0=in0, scalar1=s1, scalar2=None, op0=op0), R, W)
    return P.aop(eng, lambda e: e.tensor_scalar(out=out, in0=in0, scalar1=s1, scalar2=s2, op0=op0, op1=op1), R, W)


def TT(P, eng, out, in0, in1, op, R, W):
    return P.aop(eng, lambda e: e.tensor_tensor(out=out, in0=in0, in1=in1, op=op), R, W)


def STT(P, eng, out, in0, scalar, in1, op0, op1, R, W):
    return P.aop(eng, lambda e: e.scalar_tensor_tensor(out=out, in0=in0, scalar=scalar, in1=in1, op0=op0, op1=op1), R, W)


def DMA(P, eng, out, in_, R, W, name="d"):
    return P.adma(eng, lambda e: e.dma_start(out=out, in_=in_), R, W, name=name)


def setup_common(nc, P, es, cfg, tag, nact, nxs=2, with_sg=True, wslot=8192):
    D = cfg["D"]
    KT = D // 128
    C = Ctx()
    C.cfg = cfg
    sb = lambda name, shape, dt: es.enter_context(nc.sbuf_tensor(f"{tag}_{name}", shape, dt))
    C.sb = sb
    C.hT = sb("hT", [128, KT, CHK], BF16)
    C.act = sb("act", [128, nact, CHK], BF16)
    C.wslot = wslot
    C.wreg = sb("wreg", [128, 4 * wslot], BF16)
    C.xs = [P.slot(f"xs{i}", sb(f"xs{i}", [128, D], F32)) for i in range(nxs)]
    C.hb = sb("hb", [128, D], BF16)
    C.ss = sb("ss", [128, 8], F32)
    C.sg = [sb(f"sg{i}", [128, CHK], F32) for i in range(2)] if with_sg else None
    C.xres = [P.slot(f"xres{i}", sb(f"xres{i}", [128, 512], F32)) for i in range(4)]
    C.ost = [P.slot(f"ost{i}", sb(f"ost{i}", [128, 512], F32)) for i in range(4)]
    C.ident = sb("ident_sb", [128, 128], BF16)
    C.gcols = sb("gcols", [128, 3, KT], F32)
    C.cslot = P.slot("consts", None)
    C.wgu = [P.slot(f"wgu{i}", C.wreg[:, i * wslot:(i + 1) * wslot]) for i in range(4)]
    KG = cfg["KG"]
    C.wd = [P.slot(f"wd{i}", C.wreg[:, i * KG * 512:(i + 1) * KG * 512]) for i in range(2)]
    ps = lambda name, shape, dt: es.enter_context(nc.psum_tensor(f"{tag}_{name}", shape, dt))
    C.ps = [ps(f"ps{i}", [128, 512], F32) for i in range(6)]
    C.pT = [ps(f"pT{i}", [128, 1024], BF16) for i in range(2)]
    C.ps_free = [None] * 6
    C.pT_free = [None] * 2
    C.hT_free = None
    C.act_free = None
    C.hb_free = None
    C.sg_free = [None, None]
    C.wreg_free = None
    C.nss = 0
    C.wu_i = 0
    C.wd_i = 0
    return C


def load_consts(P, C, ident_d, gn_d):
    P.dma("sp", C.cslot, lambda e: e.dma_start(out=C.ident[:], in_=ident_d[:, :]))
    return P.dma("sp", C.cslot, lambda e: e.dma_start(out=C.gcols[:], in_=gn_d[:, :, :]))


def norm_transpose(P, C, src, r0, gi, consts_tok):
    D = C.cfg["D"]
    KT = D // 128
    last = None
    ng = (KT + 7) // 8
    for m in range(4):
        sl = C.xs[m % len(C.xs)]
        rows = src[r0 + m * 128:r0 + (m + 1) * 128, :]
        t_x = P.dma("sp", sl, lambda e, sl=sl, rows=rows: e.dma_start(out=sl.buf[:], in_=rows), deps=sl.free)
        col = C.nss % 8
        C.nss += 1
        sc = C.ss[:, col:col + 1]
        t_sq = P.op("act", lambda e, sl=sl, sc=sc: e.activation(out=C.hb[:], in_=sl.buf[:], func=AF.Square, accum_out=sc),
                    deps=[t_x, C.hb_free], sig=True)
        t1 = P.op("dve", lambda e, sc=sc: e.tensor_scalar(out=sc, in0=sc, scalar1=1.0 / D, scalar2=EPS,
                                                         op0=ALU.mult, op1=ALU.add), deps=[t_sq], sig=True)
        t2 = P.op("act", lambda e, sc=sc: e.activation(out=sc, in_=sc, func=AF.Sqrt), deps=[t1], sig=True)
        t3 = P.op("dve", lambda e, sc=sc: e.reciprocal(out=sc, in_=sc), deps=[t2], sig=True)
        t_hb = P.op("act", lambda e, sl=sl, sc=sc: e.activation(out=C.hb[:], in_=sl.buf[:], func=AF.Copy, scale=sc),
                    deps=[t3], sig=True)
        sl.free = [t_hb]
        t_tr = None
        for g0 in range(0, KT, 8):
            gi_ = (g0 // 8 + m * ng) % 2
            pT = C.pT[gi_]
            n = min(8, KT - g0)
            for kk in range(n):
                kt = g0 + kk
                o_ap = pT[:, kk * 128:(kk + 1) * 128]
                i_ap = C.hb[:, kt * 128:(kt + 1) * 128]
                t_tr = P.op("pe", lambda e, o_ap=o_ap, i_ap=i_ap: e.transpose(out=o_ap, in_=i_ap, identity=C.ident[:]),
                            deps=[t_hb, C.pT_free[gi_], consts_tok] if kk == 0 else [], sig=(kk == n - 1))
            t_ev = None
            for kk in range(n):
                kt = g0 + kk
                o_ap = C.hT[:, kt, m * 128:(m + 1) * 128]
                i_ap = pT[:, kk * 128:(kk + 1) * 128]
                g_ap = C.gcols[:, gi, kt:kt + 1]
                t_ev = P.op("dve", lambda e, o_ap=o_ap, i_ap=i_ap, g_ap=g_ap: e.tensor_scalar(
                    out=o_ap, in0=i_ap, scalar1=g_ap, scalar2=None, op0=ALU.mult),
                    deps=[t_tr, C.hT_free] if kk == 0 else [], sig=(kk == n - 1))
            C.pT_free[gi_] = t_ev
            last = t_ev
        C.hb_free = t_tr
    return last


def down_proj(P, C, actbuf, ts, nt, Wdv, scale, src, dst, r0, t_act_ready, gu_last, store_toks):
    cfg = C.cfg
    D, KG = cfg["D"], cfg["KG"]
    NCH = D // 512
    groups = [(g0, min(KG, nt - g0)) for g0 in range(0, nt, KG)]
    t_pe = None
    for n in range(NCH):
        t_po = [None] * 4
        for gidx, (g0, gn) in enumerate(groups):
            ws = C.wd[C.wd_i % 2]
            C.wd_i += 1
            o_ap = ws.buf[:, 0:gn * 512].rearrange("p (k n) -> p k n", n=512)
            i_ap = Wdv[:, ts + g0:ts + g0 + gn, n * 512:(n + 1) * 512]
            t_wd = P.dma("pool", ws, lambda e, o_ap=o_ap, i_ap=i_ap: e.dma_start(out=o_ap, in_=i_ap),
                         deps=ws.free + [gu_last, C.wreg_free])
            wv = o_ap
            for m in range(4):
                po = C.ps[m]
                for kk in range(gn):
                    first = (gidx == 0 and kk == 0)
                    lastk = (gidx == len(groups) - 1 and kk == gn - 1)
                    l_ap = actbuf[:, g0 + kk, m * 128:(m + 1) * 128]
                    r_ap = wv[:, kk, :]
                    t_pe = P.op("pe", lambda e, po=po, l_ap=l_ap, r_ap=r_ap, first=first, lastk=lastk: e.matmul(
                        po[:], lhsT=l_ap, rhs=r_ap, start=first, stop=lastk),
                        deps=([t_wd, t_act_ready] + ([C.ps_free[m]] if first else [])) if kk == 0 else [],
                        sig=(lastk or kk == gn - 1))
                if gidx == len(groups) - 1:
                    t_po[m] = t_pe
            ws.free = [t_pe]
        for m in range(4):
            xr, os_ = C.xres[m], C.ost[m]
            s_ap = src[r0 + m * 128:r0 + (m + 1) * 128, n * 512:(n + 1) * 512]
            d_ap = dst[r0 + m * 128:r0 + (m + 1) * 128, n * 512:(n + 1) * 512]
            key = (r0, m, n)
            t_xr = P.dma("sp", xr, lambda e, xr=xr, s_ap=s_ap: e.dma_start(out=xr.buf[:], in_=s_ap),
                         deps=xr.free + [store_toks.get((id(src), key))])
            t_o = P.op("dve", lambda e, po=C.ps[m], xr=xr, os_=os_: e.scalar_tensor_tensor(
                out=os_.buf[:], in0=po[:], scalar=scale, in1=xr.buf[:], op0=ALU.mult, op1=ALU.add),
                deps=[t_po[m], t_xr] + os_.free, sig=True)
            C.ps_free[m] = t_o
            xr.free = [t_o]
            t_st = P.dma("sp", os_, lambda e, os_=os_, d_ap=d_ap: e.dma_start(out=d_ap, in_=os_.buf[:]), deps=[t_o])
            os_.free = [t_st]
            store_toks[(id(dst), key)] = t_st
    return t_pe


def ffn(P, C, xsrc, xtmp, xdst, gi, Wg, Wu, Wd, consts_tok, store_toks):
    cfg = C.cfg
    D = cfg["D"]
    KT = D // 128
    halves = cfg["HALVES"]
    Wgv = Wg.rearrange("(kt p) n -> p kt n", p=128)
    Wuv = Wu.rearrange("(kt p) n -> p kt n", p=128)
    Wdv = Wd.rearrange("(ft p) n -> p ft n", p=128)
    bi = 0
    for c in range(2):
        r0 = c * CHK
        t_hT = norm_transpose(P, C, xsrc, r0, gi, consts_tok)
        t_last_pe = None
        for hf, (ts, nt) in enumerate(halves):
            src = xsrc if hf == 0 else xtmp
            dst = xdst if hf == len(halves) - 1 else xtmp
            t_act_last = None
            for u in range(nt // 2):
                sg_, su_ = C.wgu[(C.wu_i % 2) * 2], C.wgu[(C.wu_i % 2) * 2 + 1]
                C.wu_i += 1
                c0 = (ts + 2 * u) * 128
                gv = sg_.buf[:, 0:KT * 256].rearrange("p (k n) -> p k n", n=256)
                uv = su_.buf[:, 0:KT * 256].rearrange("p (k n) -> p k n", n=256)
                gi_ap = Wgv[:, :, c0:c0 + 256]
                ui_ap = Wuv[:, :, c0:c0 + 256]
                t_wg = P.dma("pool", sg_, lambda e, gv=gv, gi_ap=gi_ap: e.dma_start(out=gv, in_=gi_ap),
                             deps=sg_.free + [C.wreg_free])
                t_wu = P.dma("pool", su_, lambda e, uv=uv, ui_ap=ui_ap: e.dma_start(out=uv, in_=ui_ap),
                             deps=su_.free + [C.wreg_free])
                for jj in range(2):
                    jl = 2 * u + jj
                    ig, iu = bi * 2, bi * 2 + 1
                    pg, pu = C.ps[ig], C.ps[iu]
                    sgb = C.sg[bi]
                    for kt in range(KT):
                        l_ap = gv[:, kt, jj * 128:(jj + 1) * 128]
                        r_ap = C.hT[:, kt, :]
                        t_g = P.op("pe", lambda e, pg=pg, l_ap=l_ap, r_ap=r_ap, kt=kt: e.matmul(
                            pg[:], lhsT=l_ap, rhs=r_ap, start=(kt == 0), stop=(kt == KT - 1)),
                            deps=[t_wg, t_hT, C.ps_free[ig]] if kt == 0 else [], sig=(kt == KT - 1))
                    for kt in range(KT):
                        l_ap = uv[:, kt, jj * 128:(jj + 1) * 128]
                        r_ap = C.hT[:, kt, :]
                        t_u = P.op("pe", lambda e, pu=pu, l_ap=l_ap, r_ap=r_ap, kt=kt: e.matmul(
                            pu[:], lhsT=l_ap, rhs=r_ap, start=(kt == 0), stop=(kt == KT - 1)),
                            deps=[t_wu, C.ps_free[iu]] if kt == 0 else [], sig=(kt == KT - 1))
                    t_sg = P.op("act", lambda e, pg=pg, sgb=sgb: e.activation(out=sgb[:], in_=pg[:], func=AF.Silu),
                                deps=[t_g, C.sg_free[bi]], sig=True)
                    C.ps_free[ig] = t_sg
                    a_ap = C.act[:, jl, :]
                    t_a = P.op("dve", lambda e, pu=pu, sgb=sgb, a_ap=a_ap: e.tensor_tensor(
                        out=a_ap, in0=sgb[:], in1=pu[:], op=ALU.mult),
                        deps=[t_sg, t_u, C.act_free], sig=True)
                    C.ps_free[iu] = t_a
                    C.sg_free[bi] = t_a
                    t_act_last = t_a
                    t_last_pe = t_u
                    bi ^= 1
                sg_.free = [t_last_pe]
                su_.free = [t_last_pe]
            t_pe = down_proj(P, C, C.act, ts, nt, Wdv, 0.5, src, dst, r0, t_act_last, t_last_pe, store_toks)
            C.act_free = t_pe
            C.wreg_free = t_pe
        C.hT_free = t_last_pe


def offsets(cfg):
    CH, D = cfg["CH"], cfg["D"]
    o = {}
    o["q"] = 0
    o["kc"] = 2048
    o["vc"] = 2560
    o["ks"] = 3072
    o["vs"] = 3584
    o["kw"] = 4096
    o["vw"] = 4608
    o["gn"] = 5120
    o["ua"] = 5168
    o["ug"] = 5168 + CH
    o["ga"] = 5168 + 2 * CH
    o["gb"] = 5168 + 2 * CH + D
    o["nin"] = 5168 + 2 * CH + 2 * D
    return o


def payload_rows(cfg):
    return 3072 + cfg["CH"] // 32


def payload_views(pl, cfg):
    CH = cfg["CH"]
    v = {}
    v["ksT"] = pl[0:512, :]
    v["kwT"] = pl[512:1024, :]
    v["kcT"] = pl[1024:1536, :]
    v["vcT"] = pl[1536:2048, :]
    v["vs"] = pl[2048:2560, :].rearrange("a (b c) -> (a b) c", b=2)
    v["vw"] = pl[2560:3072, :].rearrange("a (b c) -> (a b) c", b=2)
    v["uh"] = pl[3072:3072 + CH // 32, :].rearrange("a (b c) -> (a b) c", c=32)
    return v


def load_w256(P, C, Wv, c0, ncols, key):
    KT = C.cfg["D"] // 128
    i = C.wu_i % 4
    C.wu_i += 1
    k = f"wslot{i}"
    view = C.wreg[:, i * C.wslot:i * C.wslot + KT * ncols].rearrange("p (k n) -> p k n", n=ncols)
    DMA(P, "pool", view, Wv[:, :, c0:c0 + ncols], [], [k], name="w")
    return view, k


def proj_fm(P, C, wv, wk, jj, bank, bk):
    KT = C.cfg["D"] // 128
    for kt in range(KT):
        MM(P, bank[:], wv[:, kt, jj * 128:(jj + 1) * 128], C.hT[:, kt, :], kt == 0, kt == KT - 1, [wk, "hT"], [bk])


def proj_tm(P, C, wv, wk, ncols, m, bank, bk):
    KT = C.cfg["D"] // 128
    for kt in range(KT):
        MM(P, bank[:, 0:ncols], C.hT[:, kt, m * 128:(m + 1) * 128], wv[:, kt, 0:ncols], kt == 0, kt == KT - 1,
           [wk, "hT"], [bk])


def head_norm_T(P, C, M, bank, bk, nh, gcol_idx, dst_of_head):
    sq = M.sq
    ACT(P, sq[:, 0:nh * 128], bank[:, 0:nh * 128], AF.Square, [bk], ["sq"])
    P.aop("dve", lambda e: e.tensor_reduce(out=M.hs[:, 0:nh], in_=sq[:, 0:nh * 128].rearrange("p (h d) -> p h d", d=128),
                                           axis=AX.X, op=ALU.add), ["sq"], ["hs"])
    TS(P, "dve", M.hs[:, 0:nh], M.hs[:, 0:nh], 1.0 / 128, EPS, ALU.mult, ALU.add, [], ["hs"])
    ACT(P, M.hs[:, 0:nh], M.hs[:, 0:nh], AF.Sqrt, [], ["hs"])
    P.aop("dve", lambda e: e.reciprocal(out=M.hs[:, 0:nh], in_=M.hs[:, 0:nh]), [], ["hs"])
    for h in range(nh):
        ACT(P, M.qn[:, h * 128:(h + 1) * 128], bank[:, h * 128:(h + 1) * 128], AF.Copy, [bk, "hs"], ["qn"],
            scale=M.hs[:, h:h + 1])
    for h in range(nh):
        TR(P, C.pT[0][:, h * 128:(h + 1) * 128], M.qn[:, h * 128:(h + 1) * 128], C.ident[:], ["qn"], ["pT0"])
    for h in range(nh):
        TS(P, "dve", dst_of_head(h), C.pT[0][:, h * 128:(h + 1) * 128], M.qkg[:, gcol_idx:gcol_idx + 1], None,
           ALU.mult, None, ["pT0", "qkg"], ["stg"])


def mixer_proj(P, nc, es, cfg, io):
    D, CH = cfg["D"], cfg["CH"]
    CT = CH // 128
    off = offsets(cfg)
    with ExitStack() as es2:
        C = setup_common(nc, P, es2, cfg, "m1", 1, nxs=2, with_sg=False)
        M = Ctx()
        sb = C.sb
        M.sq = sb("sq", [128, 256], F32)
        M.hs = sb("hs", [128, 2], F32)
        M.qn = sb("qn", [128, 256], BF16)
        M.qkg = sb("qkg", [128, 4], F32)
        M.stgT = sb("stgT", [128, 2, 512], BF16)
        M.stgV = sb("stgV", [128, 256], BF16)
        M.stgG = sb("stgG", [128, 48], F32)
        M.sig = sb("sig", [128, 512], F32)
        consts_tok = load_consts(P, C, io["ident"], io["gcols"])
        DMA(P, "sp", M.qkg[:], io["qkg"][:, :], [], ["qkg"])
        Wv = io["w_in"].rearrange("(kt p) n -> p kt n", p=128)
        pv = payload_views(io["payload"], cfg)
        banks = [(C.ps[i], f"ps{i}") for i in range(6)]
        bi = 0

        def nb():
            nonlocal bi
            b = banks[bi % 6]
            bi += 1
            return b

        for c in range(2):
            r0 = c * CHK
            P.barrier()
            C.hT_free = None
            t_hT = norm_transpose(P, C, io["x1"], r0, 1, consts_tok)
            P.lastw["hT"] = t_hT
            P.barrier()
            for (name, gcol, ngroups) in (("q", 0, 8), ("ks", 2, 2), ("kw", 3, 2)):
                for g in range(ngroups):
                    wv, wk = load_w256(P, C, Wv, off[name] + g * 256, 256, None)
                    for m in range(4):
                        bank, bk = nb()
                        proj_tm(P, C, wv, wk, 256, m, bank, bk)
                        head_norm_T(P, C, M, bank, bk, 2, gcol, lambda h, m=m: M.stgT[:, h, m * 128:(m + 1) * 128])
                    for h in range(2):
                        hh = g * 2 + h
                        if name == "q":
                            dst = io["qT_s"][hh, :, r0:r0 + CHK]
                        elif name == "ks":
                            dst = pv["ksT"][hh * 128:(hh + 1) * 128, r0:r0 + CHK]
                        else:
                            dst = pv["kwT"][hh * 128:(hh + 1) * 128, r0:r0 + CHK]
                        DMA(P, "sp", dst, M.stgT[:, h, :], ["stg"], [], name="st")
            for name in ("kc", "vc"):
                for g in range(2):
                    wv, wk = load_w256(P, C, Wv, off[name] + g * 256, 256, None)
                    for jj in range(2):
                        bank, bk = nb()
                        proj_fm(P, C, wv, wk, jj, bank, bk)
                        ACT(P, M.stgT[:, jj, :], bank[:], AF.Copy, [bk], ["stg"])
                        hh = g * 2 + jj
                        dst = pv[name + "T"][hh * 128:(hh + 1) * 128, r0:r0 + CHK]
                        DMA(P, "sp", dst, M.stgT[:, jj, :], ["stg"], [], name="st")
            for name in ("vs", "vw"):
                for g in range(2):
                    wv, wk = load_w256(P, C, Wv, off[name] + g * 256, 256, None)
                    for m in range(4):
                        bank, bk = nb()
                        proj_tm(P, C, wv, wk, 256, m, bank, bk)
                        ACT(P, M.stgV[:], bank[:, 0:256], AF.Copy, [bk], ["stgV"])
                        dst = pv[name][r0 + m * 128:r0 + (m + 1) * 128, g * 256:(g + 1) * 256]
                        DMA(P, "sp", dst, M.stgV[:], ["stgV"], [], name="st")
            wv, wk = load_w256(P, C, Wv, off["gn"], 48, None)
            for m in range(4):
                bank, bk = nb()
                proj_tm(P, C, wv, wk, 48, m, bank, bk)
                ACT(P, M.stgG[:], bank[:, 0:48], AF.Sigmoid, [bk], ["stgG"])
                DMA(P, "sp", io["gates_s"][r0 + m * 128:r0 + (m + 1) * 128, :], M.stgG[:], ["stgG"], [], name="st")
            for ct in range(CT):
                wa, wak = load_w256(P, C, Wv, off["ua"] + ct * 128, 128, None)
                wg, wgk = load_w256(P, C, Wv, off["ug"] + ct * 128, 128, None)
                ba, bak = nb()
                proj_fm(P, C, wa, wak, 0, ba, bak)
                bg, bgk = nb()
                proj_fm(P, C, wg, wgk, 0, bg, bgk)
                ACT(P, M.sig[:], bg[:], AF.Sigmoid, [bgk], ["sig"])
                TT(P, "dve", M.stgT[:, 0, :], M.sig[:], ba[:], ALU.mult, ["sig", bak], ["stg"])
                DMA(P, "sp", io["u_s"][ct * 128:(ct + 1) * 128, r0:r0 + CHK], M.stgT[:, 0, :], ["stg"], [], name="st")
                if c == 1:
                    DMA(P, "sp", pv["uh"][ct * 128:(ct + 1) * 128, :], M.stgT[:, 0, CHK - 32:CHK], ["stg"], [], name="st")
        P.barrier()
        P.flush()


def mixer_conv(P, nc, es, cfg, io):
    CH = cfg["CH"]
    CT = CH // 128
    with ExitStack() as es2:
        sb = lambda name, shape, dt: es2.enter_context(nc.sbuf_tensor("cv_" + name, shape, dt))
        ps = lambda name, shape, dt: es2.enter_context(nc.psum_tensor("cv_" + name, shape, dt))
        ue = sb("ue", [128, CT, 32 + TOK], BF16)
        uh = sb("uh", [128, CT, 8, 32], BF16)
        onehot = sb("onehot", [128, 8], F32)
        cw = sb("cw", [128, CT, 31], F32)
        cp = sb("cp", [128, CT, 3], F32)
        cbuf = sb("cbuf", [128, CT, CHK], F32)
        sqb = sb("sqb", [128, CHK], F32)
        mu = sb("mu", [128, CHK], F32)
        rs = sb("rs", [128, CHK], F32)
        t1 = sb("t1", [128, CHK], F32)
        ones = sb("ones", [128, 128], F32)
        stg = sb("stg", [128, CHK], BF16)
        p1 = ps("p1", [128, 512], F32)
        p2 = ps("p2", [128, 512], F32)
        DMA(P, "sp", onehot[:], io["onehot"][:, :], [], ["onehot"])
        DMA(P, "sp", cw[:], io["convw"][:, :, :], [], ["cw"])
        DMA(P, "sp", cp[:], io["convp"][:, :, :], [], ["cp"])
        P.aop("pool", lambda e: e.memset(ones[:], 1.0), [], ["ones"])
        g3 = io["gathered"].rearrange("(r a) c -> r a c", r=NCORE)
        R0 = 3072
        for ct in range(CT):
            DMA(P, "sp", ue[:, ct, 32:32 + TOK], io["u_s"][ct * 128:(ct + 1) * 128, :], [], ["ue"])
            src = g3[:, R0 + ct * 4:R0 + ct * 4 + 4, :].rearrange("r a (b c) -> (a b) r c", c=32)
            DMA(P, "sp", uh[:, ct, :, :], src, [], ["uh"])
        for ct in range(CT):
            TS(P, "dve", ue[:, ct, 0:32], uh[:, ct, 0, :], onehot[:, 0:1], None, ALU.mult, None, ["uh", "onehot"], ["ue"])
            for r in range(1, 8):
                STT(P, "dve", ue[:, ct, 0:32], uh[:, ct, r, :], onehot[:, r:r + 1], ue[:, ct, 0:32], ALU.mult, ALU.add,
                    ["uh", "onehot"], ["ue"])
        for c in range(2):
            c0 = c * CHK
            for ct in range(CT):
                eng = "dve"
                key = f"cbuf{ct}"
                TS(P, eng, cbuf[:, ct, :], ue[:, ct, c0 + 2:c0 + 2 + CHK], cw[:, ct, 0:1], cp[:, ct, 0:1], ALU.mult, ALU.add,
                   ["ue", "cw", "cp"], [key])
                for k in range(1, 31):
                    STT(P, eng, cbuf[:, ct, :], ue[:, ct, c0 + 2 + k:c0 + 2 + k + CHK], cw[:, ct, k:k + 1], cbuf[:, ct, :],
                        ALU.mult, ALU.add, ["ue", "cw"], [key])
            for ct in range(CT):
                MM(P, p1[:], ones[:], cbuf[:, ct, :], ct == 0, ct == CT - 1, ["ones", f"cbuf{ct}"], ["p1"])
            for ct in range(CT):
                ACT(P, sqb[:], cbuf[:, ct, :], AF.Square, [f"cbuf{ct}"], ["sqb"])
                MM(P, p2[:], ones[:], sqb[:], ct == 0, ct == CT - 1, ["ones", "sqb"], ["p2"])
            TS(P, "dve", mu[:], p1[:], 1.0 / CH, None, ALU.mult, None, ["p1"], ["mu"])
            TT(P, "dve", t1[:], mu[:], mu[:], ALU.mult, ["mu"], ["t1"])
            STT(P, "dve", rs[:], p2[:], 1.0 / CH, t1[:], ALU.mult, ALU.subtract, ["p2", "t1"], ["rs"])
            TS(P, "dve", rs[:], rs[:], EPS, None, ALU.add, None, [], ["rs"])
            ACT(P, rs[:], rs[:], AF.Sqrt, [], ["rs"])
            P.aop("dve", lambda e: e.reciprocal(out=rs[:], in_=rs[:]), [], ["rs"])
            for ct in range(CT):
                TT(P, "dve", t1[:], cbuf[:, ct, :], mu[:], ALU.subtract, [f"cbuf{ct}", "mu"], ["t1"])
                TT(P, "dve", t1[:], t1[:], rs[:], ALU.mult, ["rs"], ["t1"])
                ACT(P, stg[:], t1[:], AF.Silu, ["t1", "cp"], ["stg"], scale=cp[:, ct, 1:2], bias=cp[:, ct, 2:3])
                DMA(P, "sp", io["convT_s"][ct * 128:(ct + 1) * 128, c0:c0 + CHK], stg[:], ["stg"], [], name="st")
        P.barrier()
        P.flush()


def mixer_compress(P, nc, es, cfg, io, K):
    with ExitStack() as es2:
        sb = lambda name, shape, dt: es2.enter_context(nc.sbuf_tensor("cp_" + name, shape, dt))
        ps = lambda name, shape, dt: es2.enter_context(nc.psum_tensor("cp_" + name, shape, dt))
        kbig = sb("kbig", [128, SEQ + 16], BF16)
        W1 = sb("W1", [128, 32, 512], BF16)
        W2 = sb("W2", [128, 4, 128], BF16)
        pos = sb("pos", [128, 2, 32], F32)
        qkg = sb("qkg", [128, 4], F32)
        ident = sb("ident", [128, 128], BF16)
        flat = [sb(f"flat{i}", [128, 512], BF16) for i in range(2)]
        hid = sb("hid", [128, 4, 512], BF16)
        x2 = sb("x2", [128, 512], F32)
        tt = sb("tt", [128, 512], F32)
        sg = sb("sg", [128, 512], F32)
        tok = sb("tok", [128, 128], F32)
        junk = sb("junk", [128, 128], F32)
        tokb = sb("tokb", [128, 128], BF16)
        ssq = sb("ssq", [128, 1], F32)
        ph = [ps(f"ph{i}", [128, 512], F32) for i in range(4)]
        po = ps("po", [128, 512], F32)
        pT = ps("pT", [128, 1024], BF16)
        DMA(P, "sp", pos[:], io["posT"][:, :, :], [], ["pos"])
        DMA(P, "sp", qkg[:], io["qkg"][:, :], [], ["qkg"])
        DMA(P, "sp", ident[:], io["ident"][:, :], [], ["ident"])
        P.aop("pool", lambda e: e.memset(kbig[:, SEQ:SEQ + 16], 0.0), [], ["kbig"])
        P.aop("pool", lambda e: e.memset(K.vcmp[:, :, :, 128:129], 1.0), [], ["vcmp"])
        g3 = io["gathered"].rearrange("(r a) c -> r a c", r=NCORE)
        for which, (w1, w2, rbase) in enumerate(((io["cmp_k_w1"], io["cmp_k_w2"], 1024), (io["cmp_v_w1"], io["cmp_v_w2"], 1536))):
            w1v = w1.rearrange("(l p) n -> p l n", p=128)
            for l0 in range(0, 32, 8):
                DMA(P, "pool", W1[:, l0:l0 + 8, :], w1v[:, l0:l0 + 8, :], [], ["W1"], name="w")
            DMA(P, "pool", W2[:], w2.rearrange("(t p) n -> p t n", p=128), [], ["W2"], name="w")
            for hk in range(4):
                for r in range(NCORE):
                    DMA(P, "sp", kbig[:, r * TOK:(r + 1) * TOK], g3[r, rbase + hk * 128:rbase + (hk + 1) * 128, :], [], ["kbig"])
                kv = kbig[:, 0:SEQ + 16].rearrange("p (j s) -> p j s", s=16)
                for l in range(32):
                    fb = flat[l % 2]
                    fk = f"flat{l % 2}"
                    src = kv[:, (l // 16):(l // 16) + 512, l % 16]
                    TS(P, "dve" if l % 2 == 0 else "pool", fb[:], src, pos[:, which, l:l + 1], None, ALU.add, None,
                       ["kbig", "pos"], [fk])
                    for ht in range(4):
                        MM(P, ph[ht][:], W1[:, l, ht * 128:(ht + 1) * 128], fb[:], l == 0, l == 31, ["W1", fk], [f"ph{ht}"])
                for ht in range(4):
                    ACT(P, x2[:], ph[ht][:], AF.Square, [f"ph{ht}"], ["x2"])
                    TS(P, "dve", tt[:], x2[:], 0.044715, 1.0, ALU.mult, ALU.add, ["x2"], ["tt"])
                    TT(P, "dve", tt[:], tt[:], ph[ht][:], ALU.mult, [f"ph{ht}"], ["tt"])
                    ACT(P, sg[:], tt[:], AF.Sigmoid, ["tt"], ["sg"], scale=1.5957691216057308)
                    TT(P, "dve", hid[:, ht, :], sg[:], ph[ht][:], ALU.mult, ["sg", f"ph{ht}"], ["hid"])
                for jt in range(4):
                    for ht in range(4):
                        MM(P, po[:, 0:128], hid[:, ht, jt * 128:(jt + 1) * 128], W2[:, ht, :], ht == 0, ht == 3,
                           ["hid", "W2"], ["po"])
                    if which == 1:
                        ACT(P, K.vcmp[:, hk, jt, 0:128], po[:, 0:128], AF.Copy, ["po"], ["vcmp"])
                    else:
                        ACT(P, junk[:], po[:, 0:128], AF.Square, ["po"], ["junk", "ssq"], accum=ssq[:, 0:1])
                        TS(P, "dve", ssq[:], ssq[:], 1.0 / 128, EPS, ALU.mult, ALU.add, [], ["ssq"])
                        ACT(P, ssq[:], ssq[:], AF.Sqrt, [], ["ssq"])
                        P.aop("dve", lambda e: e.reciprocal(out=ssq[:], in_=ssq[:]), [], ["ssq"])
                        ACT(P, tokb[:], po[:, 0:128], AF.Copy, ["po", "ssq"], ["tokb"], scale=ssq[:, 0:1])
                        TR(P, pT[:, 0:128], tokb[:], ident[:], ["tokb", "ident"], ["pT"])
                        TS(P, "dve", K.kcmpT[:, hk, jt * 128:(jt + 1) * 128], pT[:, 0:128], qkg[:, 1:2], None, ALU.mult, None,
                           ["pT", "qkg"], ["kcmpT"])
        P.barrier()
        P.flush()


def mixer_attn(P, nc, es, cfg, io, K):
    with ExitStack() as es2:
        sb = lambda name, shape, dt: es2.enter_context(nc.sbuf_tensor("at_" + name, shape, dt))
        ps = lambda name, shape, dt: es2.enter_context(nc.psum_tensor("at_" + name, shape, dt))
        ident = sb("ident", [128, 128], BF16)
        E2 = sb("E2", [128, 64, 128], BF16)
        Cov = sb("Cov", [128, 4, 128], BF16)
        cmaskb = sb("cmaskb", [128, 4, TOK], BF16)
        wmaskb = sb("wmaskb", [128, 16, CHK], BF16)
        lmaskb = sb("lmaskb", [128, 4, CHK], BF16)
        selmul = sb("selmul", [128, 8, 128], F32)
        seladd = sb("seladd", [128, 8, 128], F32)
        farmask = sb("farmask", [128, 8, 128], F32)
        onehot = sb("onehot", [128, 8], F32)
        gates = sb("gates", [128, 8, 48], F32)
        kbig = sb("kbig", [128, SEQ], BF16)
        vbig = sb("vbig", [128, 64, 129], BF16)
        kwh = sb("kwh", [128, 8, 512], BF16)
        vwh = sb("vwh", [128, 8, 4, 128], BF16)
        kwe = sb("kwe", [128, 1536], BF16)
        vwe = sb("vwe", [128, 12, 129], BF16)
        qTh = [sb(f"qTh{i}", [128, TOK], BF16) for i in range(2)]
        pTs = [sb(f"pTs{i}", [128, CHK], BF16) for i in range(6)]
        osb = sb("osb", [128, 3, 4, 129], F32)
        rden = sb("rden", [128, 3, 4], F32)
        gsc = sb("gsc", [128, 3, 4], F32)
        impa = sb("impa", [128, 4, 128], F32)
        score = sb("score", [128, 128], F32)
        sc2 = sb("sc2", [128, 128], F32)
        m8a = sb("m8a", [128, 8], F32)
        m8b = sb("m8b", [128, 8], F32)
        selb = sb("selb", [128, 128], BF16)
        selTb = sb("selTb", [128, CHK], BF16)
        atm = sb("atm", [128, 128], F32)
        atb = sb("atb", [128, 128], BF16)
        aTs = sb("aTs", [128, CHK], BF16)
        pS = [ps(f"pS{i}", [128, 512], F32) for i in range(2)]
        pA = [ps(f"pA{i}", [128, 512], F32) for i in range(2)]
        pI = ps("pI", [128, 512], F32)
        pT = ps("pT", [128, 1024], BF16)
        for (t, s, k) in ((ident, io["ident"], "ident"), (onehot, io["onehot"], "onehot")):
            DMA(P, "sp", t[:], s[:, :], [], [k])
        for (t, s, k) in ((Cov, io["Cov"], "Cov"), (cmaskb, io["cmaskb"], "cmaskb"),
                          (lmaskb, io["lmaskb"], "lmaskb"), (selmul, io["selmul"], "selmul"), (seladd, io["seladd"], "seladd"),
                          (farmask, io["farmask"], "farmask")):
            DMA(P, "sp", t[:], s[:, :, :], [], [k])
        for i0 in range(0, 64, 16):
            DMA(P, "sp", E2[:, i0:i0 + 16, :], io["E2"][:, i0:i0 + 16, :], [], ["E2"])
        for i0 in range(0, 16, 4):
            DMA(P, "sp", wmaskb[:, i0:i0 + 4, :], io["wmaskb"][:, i0:i0 + 4, :], [], ["wmaskb"])
        DMA(P, "sp", gates[:], io["gates_s"].rearrange("(t p) g -> p t g", p=128), [], ["gates"])
        P.aop("pool", lambda e: e.memset(vbig[:, :, 128:129], 1.0), [], ["vbig"])
        P.aop("pool", lambda e: e.memset(vwe[:, :, 128:129], 1.0), [], ["vwe"])
        g3 = io["gathered"].rearrange("(r a) c -> r a c", r=NCORE)
        pv = payload_views(io["payload"], cfg)
        npt = [0]

        def attend(qT, qk, tiles, acc_first, acc_last, keep=None):
            kept = []
            for ti, (kT_ap, kk, ml, mr, mk, v_ap, vk) in enumerate(tiles):
                sbank = pS[npt[0] % 2]
                sk = f"pS{npt[0] % 2}"
                pt = pTs[npt[0] % 6]
                pk = f"pTs{npt[0] % 6}"
                npt[0] += 1
                MM(P, sbank[:], kT_ap, qT, True, False, kk + [qk], [sk])
                MM(P, sbank[:], ml, mr, False, True, mk, [sk])
                ACT(P, pt[:], sbank[:], AF.Exp, [sk], [pk], scale=SCALE)
                for qs in range(4):
                    a = pA[qs // 2][:, (qs % 2) * 256:(qs % 2) * 256 + 129]
                    MM(P, a, pt[:, qs * 128:(qs + 1) * 128], v_ap, acc_first and ti == 0 and qs % 2 == 0,
                       acc_last and ti == len(tiles) - 1,
                       [pk] + vk, [f"pA{qs // 2}"])
                kept.append((pt, pk))
            return kept

        def evac(b):
            for qs in range(4):
                a = pA[qs // 2][:, (qs % 2) * 256:(qs % 2) * 256 + 129]
                ACT(P, osb[:, b, qs, :], a, AF.Copy, [f"pA{qs // 2}"], ["osb"])
            TS(P, "dve", rden[:, b, :], osb[:, b, :, 128], 1e-30, None, ALU.add, None, ["osb"], ["rden"])
            P.aop("dve", lambda e: e.reciprocal(out=rden[:, b, :], in_=rden[:, b, :]), [], ["rden"])

        hq = 0
        for hk in range(4):
            for r in range(NCORE):
                DMA(P, "sp", kbig[:, r * TOK:(r + 1) * TOK], g3[r, hk * 128:(hk + 1) * 128, :], [], ["kbig"])
                vsrc = g3[r, 2048:2560, :].rearrange("a (b c) -> (a b) c", b=2)[:, hk * 128:(hk + 1) * 128]
                DMA(P, "sp", vbig[:, r * 8:(r + 1) * 8, 0:128], vsrc.rearrange("(t p) c -> p t c", p=128), [], ["vbig"])
                DMA(P, "sp", kwh[:, r, :], g3[r, 512 + hk * 128:512 + (hk + 1) * 128, 512:1024], [], ["kwh"])
                wsrc = g3[r, 2560:3072, :].rearrange("a (b c) -> (a b) c", b=2)[512:1024, hk * 128:(hk + 1) * 128]
                DMA(P, "sp", vwh[:, r, :, :], wsrc.rearrange("(t p) c -> p t c", p=128), [], ["vwh"])
            DMA(P, "sp", kwe[:, 512:1536], pv["kwT"][hk * 128:(hk + 1) * 128, :], [], ["kwe"])
            DMA(P, "sp", vwe[:, 4:12, 0:128], pv["vw"][:, hk * 128:(hk + 1) * 128].rearrange("(t p) c -> p t c", p=128),
                [], ["vwe"])
            TS(P, "dve", kwe[:, 0:512], kwh[:, 0, :], onehot[:, 0:1], None, ALU.mult, None, ["kwh", "onehot"], ["kwe"])
            TS(P, "dve", vwe[:, 0:4, 0:128], vwh[:, 0, :, :], onehot[:, 0:1], None, ALU.mult, None, ["vwh", "onehot"], ["vwe"])
            for r in range(1, 8):
                STT(P, "dve", kwe[:, 0:512], kwh[:, r, :], onehot[:, r:r + 1], kwe[:, 0:512], ALU.mult, ALU.add,
                    ["kwh", "onehot"], ["kwe"])
                STT(P, "dve", vwe[:, 0:4, 0:128], vwh[:, r, :, :], onehot[:, r:r + 1], vwe[:, 0:4, 0:128], ALU.mult, ALU.add,
                    ["vwh", "onehot"], ["vwe"])
            for c in range(2):
                c0 = c * CHK
                for g in range(4):
                    h = hk * 4 + g
                    qb = qTh[hq % 2]
                    qk = f"qTh{hq % 2}"
                    hq += 1
                    DMA(P, "sp", qb[:, 0:CHK], io["qT_s"][h, :, c0:c0 + CHK], [], [qk])
                    qT = qb[:, 0:CHK]
                    tiles = [(K.kcmpT[:, hk, jt * 128:(jt + 1) * 128], ["kcmpT"], ident[:], cmaskb[:, jt, c0:c0 + CHK],
                              ["ident", "cmaskb"], K.vcmp[:, hk, jt, :], ["vcmp"]) for jt in range(4)]
                    kept = attend(qT, qk, tiles, True, True)
                    evac(0)
                    for qs in range(4):
                        for jt in range(4):
                            pt, pk = kept[jt]
                            MM(P, pI[:, qs * 128:(qs + 1) * 128], pt[:, qs * 128:(qs + 1) * 128], Cov[:, jt, :], jt == 0, jt == 3,
                               [pk, "Cov"], ["pI"])
                    for qs in range(4):
                        if g == 0:
                            TS(P, "dve", impa[:, qs, :], pI[:, qs * 128:(qs + 1) * 128], rden[:, 0, qs:qs + 1], None, ALU.mult, None,
                               ["pI", "rden"], ["impa"])
                        else:
                            STT(P, "dve", impa[:, qs, :], pI[:, qs * 128:(qs + 1) * 128], rden[:, 0, qs:qs + 1], impa[:, qs, :],
                                ALU.mult, ALU.add, ["pI", "rden"], ["impa"])
                    for qs in range(4):
                        qt = c * 4 + qs
                        TT(P, "dve", gsc[:, 0, qs:qs + 1], rden[:, 0, qs:qs + 1], gates[:, qt, h * 3:h * 3 + 1], ALU.mult,
                           ["rden", "gates"], ["gsc"])
                        TS(P, "dve", K.ocmp[:, g, qs, :], osb[:, 0, qs, 0:128], gsc[:, 0, qs:qs + 1], None, ALU.mult, None,
                           ["osb", "gsc"], ["ocmp"])
                for qs in range(4):
                    qt = c * 4 + qs
                    TT(P, "dve", score[:], impa[:, qs, :], selmul[:, qt, :], ALU.mult, ["impa", "selmul"], ["score"])
                    TT(P, "dve", score[:], score[:], seladd[:, qt, :], ALU.add, ["seladd"], ["score"])
                    P.aop("dve", lambda e: e.max(out=m8a[:], in_=score[:]), ["score"], ["m8a"])
                    P.aop("dve", lambda e: e.match_replace(out=sc2[:], in_to_replace=m8a[:], in_values=score[:], imm_value=-3.0e38),
                          ["score", "m8a"], ["sc2"])
                    P.aop("dve", lambda e: e.max(out=m8b[:], in_=sc2[:]), ["sc2"], ["m8b"])
                    TS(P, "dve", sc2[:], score[:], m8b[:, 7:8], None, ALU.is_ge, None, ["score", "m8b"], ["sc2"])
                    TT(P, "dve", sc2[:], sc2[:], farmask[:, qt, :], ALU.mult, ["farmask"], ["sc2"])
                    TS(P, "dve", selb[:], sc2[:], -1.0, -NEGB, ALU.add, ALU.mult, ["sc2"], ["selb"])
                    TR(P, pT[:, 0:128], selb[:], ident[:], ["selb", "ident"], ["pT"])
                    ACT(P, selTb[:, qs * 128:(qs + 1) * 128], pT[:, 0:128], AF.Copy, ["pT"], ["selTb"])
                for g in range(4):
                    h = hk * 4 + g
                    qb = qTh[hq % 2]
                    qk = f"qTh{hq % 2}"
                    hq += 1
                    DMA(P, "sp", qb[:, 0:CHK], io["qT_s"][h, :, c0:c0 + CHK], [], [qk])
                    qT = qb[:, 0:CHK]
                    tiles = [(kbig[:, i * 128:(i + 1) * 128], ["kbig"], E2[:, i, :], selTb[:], ["E2", "selTb"], vbig[:, i, :], ["vbig"])
                             for i in range(64)]
                    attend(qT, qk, tiles, True, False)
                    tiles = [(K.ksown[:, hk, c0 + r * 128:c0 + (r + 1) * 128], ["ksown"], ident[:], lmaskb[:, r, :],
                              ["ident", "lmaskb"], K.vsown[:, hk, c * 4 + r, :], ["vsown"]) for r in range(4)]
                    attend(qT, qk, tiles, False, True)
                    evac(1)
                    tiles = [(kwe[:, (4 * c + r) * 128:(4 * c + r + 1) * 128], ["kwe"], ident[:], wmaskb[:, c * 8 + r, :],
                              ["ident", "wmaskb"], vwe[:, 4 * c + r, :], ["vwe"]) for r in range(8)]
                    attend(qT, qk, tiles, True, True)
                    evac(2)
                    for qs in range(4):
                        qt = c * 4 + qs
                        for b in (1, 2):
                            TT(P, "dve", gsc[:, b, qs:qs + 1], rden[:, b, qs:qs + 1], gates[:, qt, h * 3 + b:h * 3 + b + 1], ALU.mult,
                               ["rden", "gates"], ["gsc"])
                        STT(P, "dve", atm[:], osb[:, 1, qs, 0:128], gsc[:, 1, qs:qs + 1], K.ocmp[:, g, qs, :], ALU.mult, ALU.add,
                            ["osb", "gsc", "ocmp"], ["atm"])
                        STT(P, "dve", atb[:], osb[:, 2, qs, 0:128], gsc[:, 2, qs:qs + 1], atm[:], ALU.mult, ALU.add,
                            ["osb", "gsc", "atm"], ["atb"])
                        TR(P, pT[:, 128:256], atb[:], ident[:], ["atb", "ident"], ["pT"])
                        ACT(P, aTs[:, qs * 128:(qs + 1) * 128], pT[:, 128:256], AF.Copy, ["pT"], ["aTs"])
                    DMA(P, "sp", io["attnT_s"][h, :, c0:c0 + CHK], aTs[:], ["aTs"], [], name="st")
        P.barrier()
        P.flush()


def mixer_merge(P, nc, es, cfg, io):
    D, CH = cfg["D"], cfg["CH"]
    KT, CT = D // 128, CH // 128
    off = offsets(cfg)
    with ExitStack() as es2:
        C = setup_common(nc, P, es2, cfg, "m3", KT, nxs=1, with_sg=False, wslot=max(cfg["KG"] * 256, KT * 128, 2048, CT * 128))
        sb = C.sb
        aT = sb("aT", [128, 16, CHK], BF16)
        cT = sb("cT", [128, CT, CHK], BF16)
        sa = sb("sa", [128, CHK], F32)
        sbb = sb("sbb", [128, CHK], F32)
        t1 = sb("t1", [128, CHK], F32)
        consts_tok = load_consts(P, C, io["ident"], io["gcols"])
        Wv = io["w_in"].rearrange("(kt p) n -> p kt n", p=128)
        Wnv = io["nsa_w_o"].rearrange("(kt p) n -> p kt n", p=128)
        Wcv = io["conv_w_o"].rearrange("(kt p) n -> p kt n", p=128)
        Wov = io["w_out"].rearrange("(kt p) n -> p kt n", p=128)
        store_toks = {}
        for c in range(2):
            r0 = c * CHK
            P.barrier()
            C.hT_free = None
            C.ps_free = [None] * 6
            C.wreg_free = None
            for s in C.wd + C.xres + C.ost + C.xs:
                s.free = []
            t_hT = norm_transpose(P, C, io["x1"], r0, 1, consts_tok)
            P.lastw["hT"] = t_hT
            P.barrier()
            for h in range(16):
                DMA(P, "sp", aT[:, h, :], io["attnT_s"][h, :, r0:r0 + CHK], [], ["aT"])
            for ct in range(CT):
                DMA(P, "sp", cT[:, ct, :], io["convT_s"][ct * 128:(ct + 1) * 128, r0:r0 + CHK], [], ["cT"])
            for f in range(KT):
                wa, wak = load_w256(P, C, Wv, off["ga"] + f * 128, 128, None)
                wb, wbk = load_w256(P, C, Wv, off["gb"] + f * 128, 128, None)
                i = C.wu_i % 4
                C.wu_i += 1
                wnk = f"wslot{i}"
                wn = C.wreg[:, i * C.wslot:i * C.wslot + 16 * 128].rearrange("p (k n) -> p k n", n=128)
                DMA(P, "pool", wn, Wnv[:, :, f * 128:(f + 1) * 128], [], [wnk], name="w")
                i = C.wu_i % 4
                C.wu_i += 1
                wck = f"wslot{i}"
                wc = C.wreg[:, i * C.wslot:i * C.wslot + CT * 128].rearrange("p (k n) -> p k n", n=128)
                DMA(P, "pool", wc, Wcv[:, :, f * 128:(f + 1) * 128], [], [wck], name="w")
                proj_fm(P, C, wa, wak, 0, C.ps[0], "ps0")
                proj_fm(P, C, wb, wbk, 0, C.ps[1], "ps1")
                for kt in range(16):
                    MM(P, C.ps[2][:], wn[:, kt, :], aT[:, kt, :], kt == 0, kt == 15, [wnk, "aT"], ["ps2"])
                for kt in range(CT):
                    MM(P, C.ps[3][:], wc[:, kt, :], cT[:, kt, :], kt == 0, kt == CT - 1, [wck, "cT"], ["ps3"])
                ACT(P, sa[:], C.ps[0][:], AF.Sigmoid, ["ps0"], ["sa"])
                ACT(P, sbb[:], C.ps[1][:], AF.Sigmoid, ["ps1"], ["sbb"])
                TT(P, "dve", t1[:], sa[:], C.ps[2][:], ALU.mult, ["sa", "ps2"], ["t1"])
                TT(P, "dve", sbb[:], sbb[:], C.ps[3][:], ALU.mult, ["ps3"], ["sbb"])
                TT(P, "dve", C.act[:, f, :], t1[:], sbb[:], ALU.add, ["t1", "sbb"], ["mT"])
            P.barrier()
            down_proj(P, C, C.act, 0, KT, Wov, 1.0, io["x1"], io["x2"], r0, None, None, store_toks)
        P.barrier()
        P.flush()


def declare_io(nc, cfg, part):
    D, FF, CH = cfg["D"], cfg["FF"], cfg["CH"]
    KT, CT = D // 128, CH // 128
    off = offsets(cfg)
    R = payload_rows(cfg)
    io = {}

    def inp(name, shape, dt):
        io[name] = nc.dram_tensor(name, shape, dt, kind="ExternalInput").ap()

    def mid(name, shape, dt, producer):
        if part == "all":
            io[name] = nc.dram_tensor(name, shape, dt).ap()
        elif part == producer:
            io[name] = nc.dram_tensor(name, shape, dt, kind="ExternalOutput").ap()
        elif part == "B" and producer == "A":
            io[name] = nc.dram_tensor(name, shape, dt, kind="ExternalInput").ap()

    inp("ident", [128, 128], BF16)
    inp("gcols", [128, 3, KT], F32)
    inp("qkg", [128, 4], F32)
    if part in ("all", "A"):
        inp("x", [TOK, D], F32)
        inp("wg1", [D, FF], F32)
        inp("wu1", [D, FF], F32)
        inp("wd1", [FF, D], F32)
    inp("w_in", [D, off["nin"]], F32)
    mid("x1", [TOK, D], F32, "A")
    mid("payload", [R, 1024], BF16, "A")
    mid("qT_s", [16, 128, TOK], BF16, "A")
    mid("gates_s", [TOK, 48], F32, "A")
    mid("u_s", [CH, TOK], BF16, "A")
    if part == "all":
        io["gathered"] = nc.dram_tensor("gathered", [NCORE * R, 1024], BF16).ap()
    elif part == "B":
        inp("gathered", [NCORE * R, 1024], BF16)
    if part in ("all", "B"):
        for nm, shp, dt in (("onehot", [128, 8], F32), ("convw", [128, CT, 31], F32), ("convp", [128, CT, 3], F32),
                            ("posT", [128, 2, 32], F32), ("E2", [128, 64, 128], BF16), ("Cov", [128, 4, 128], BF16),
                            ("cmaskb", [128, 4, TOK], BF16), ("wmaskb", [128, 16, CHK], BF16), ("lmaskb", [128, 4, CHK], BF16),
                            ("selmul", [128, 8, 128], F32), ("seladd", [128, 8, 128], F32), ("farmask", [128, 8, 128], F32),
                            ("cmp_k_w1", [4096, 512], F32), ("cmp_k_w2", [512, 128], F32),
                            ("cmp_v_w1", [4096, 512], F32), ("cmp_v_w2", [512, 128], F32),
                            ("nsa_w_o", [2048, D], F32), ("conv_w_o", [CH, D], F32), ("w_out", [D, D], F32),
                            ("wg2", [D, FF], F32), ("wu2", [D, FF], F32), ("wd2", [FF, D], F32)):
            inp(nm, shp, dt)
        dk = dict(kind="ExternalOutput") if DEBUG else {}
        io["convT_s"] = nc.dram_tensor("convT_s", [CH, TOK], BF16, **dk).ap()
        io["attnT_s"] = nc.dram_tensor("attnT_s", [16, 128, TOK], BF16, **dk).ap()
        io["x2"] = nc.dram_tensor("x2", [TOK, D], F32, **dk).ap()
        io["out"] = nc.dram_tensor("out", [TOK, D], F32, kind="ExternalOutput").ap()
    io["tmp"] = nc.dram_tensor("tmp", [TOK, D], F32).ap()
    return io


def ffn_phase(P, nc, cfg, io, tag, xsrc, xdst, gi, wg, wu, wd, final):
    with ExitStack() as es2:
        C = setup_common(nc, P, es2, cfg, tag, cfg["FHMAX"])
        t_c = load_consts(P, C, io["ident"], io["gcols"])
        store_toks = {}
        ffn(P, C, xsrc, io["tmp"], xdst, gi, wg, wu, wd, t_c, store_toks)
        fw = None
        if final:
            fw = {"sp": [("slot:" + s.name, s.cnt) for s in C.ost if s.cnt]}
        P.barrier()
        P.flush(fw)


def build(cfg, part):
    nc = bass.Bass("TRN2", target_bir_lowering=False)
    io = declare_io(nc, cfg, part)
    with ExitStack() as es:
        P = Prog(nc, es)
        if part in ("all", "A"):
            ffn_phase(P, nc, cfg, io, "f1", io["x"], io["x1"], 0, io["wg1"], io["wu1"], io["wd1"], False)
            mixer_proj(P, nc, es, cfg, io)
        if part == "all":
            s = P.slot("cc")
            s.name = "cc_" + s.name
            P.dma("pool", s, lambda e: e.collective_compute("AllGather", ALU.bypass, replica_groups=[list(range(NCORE))],
                                                           ins=[io["payload"][:, :]], outs=[io["gathered"][:, :]]))
            s.cnt -= 15
            P._waits("pool", [("slot:" + s.name, 1)])
            P.barrier()
            P.flush()
        if part in ("all", "B"):
            mixer_conv(P, nc, es, cfg, io)
            esK = ExitStack()
            K = Ctx()
            K.kcmpT = esK.enter_context(nc.sbuf_tensor("K_kcmpT", [128, 4, 512], BF16))
            K.vcmp = esK.enter_context(nc.sbuf_tensor("K_vcmp", [128, 4, 4, 129], BF16))
            K.ksown = esK.enter_context(nc.sbuf_tensor("K_ksown", [128, 4, TOK], BF16))
            K.vsown = esK.enter_context(nc.sbuf_tensor("K_vsown", [128, 4, 8, 129], BF16))
            K.ocmp = esK.enter_context(nc.sbuf_tensor("K_ocmp", [128, 4, 4, 128], F32))
            pv = payload_views(io["payload"], cfg)
            for hk in range(4):
                DMA(P, "sp", K.ksown[:, hk, :], pv["ksT"][hk * 128:(hk + 1) * 128, :], [], ["ksown"])
                DMA(P, "sp", K.vsown[:, hk, :, 0:128],
                    pv["vs"][:, hk * 128:(hk + 1) * 128].rearrange("(t p) c -> p t c", p=128), [], ["vsown"])
            P.aop("pool", lambda e: e.memset(K.vsown[:, :, :, 128:129], 1.0), [], ["vsown"])
            mixer_compress(P, nc, es, cfg, io, K)
            mixer_attn(P, nc, es, cfg, io, K)
            esK.close()
            mixer_merge(P, nc, es, cfg, io)
            ffn_phase(P, nc, cfg, io, "f2", io["x2"], io["out"], 2, io["wg2"], io["wu2"], io["wd2"], True)
    return nc


def host_constants(cfg, core):
    import ml_dtypes
    bf = ml_dtypes.bfloat16
    c = {}
    base = core * TOK
    t = base + np.arange(TOK)
    j = np.arange(512)
    cmp_end = j * 16 + 31
    valid = (cmp_end[:, None] <= t[None, :]) & (j[:, None] <= 510)
    cm = np.where(valid, 0.0, NEGB).astype(np.float32)
    c["cmaskb"] = np.ascontiguousarray(cm.reshape(4, 128, TOK).transpose(1, 0, 2)).astype(bf)
    wm = np.zeros((128, 16, CHK), np.float32)
    for ch in range(2):
        for r in range(8):
            kpos = base + ch * CHK - 512 + r * 128 + np.arange(128)
            qpos = base + ch * CHK + np.arange(CHK)
            ok = (kpos[:, None] >= 0) & (kpos[:, None] <= qpos[None, :]) & (qpos[None, :] - kpos[:, None] < 512)
            wm[:, ch * 8 + r, :] = np.where(ok, 0.0, NEGB)
    c["wmaskb"] = wm.astype(bf)
    s = np.arange(128)
    cur = t // 64
    forced = (s[None, :] == 0) | (s[None, :] == cur[:, None]) | (s[None, :] == cur[:, None] - 1)
    validb = (s[None, :] * 64) <= t[:, None]
    selmul = (validb & ~forced).astype(np.float32)
    seladd = np.where(forced, 1e9, np.where(validb, 0.0, -1e30)).astype(np.float32)
    far = (s[None, :] < cur[:, None]).astype(np.float32)
    for nm, a in (("selmul", selmul), ("seladd", seladd), ("farmask", far)):
        c[nm] = np.ascontiguousarray(a.reshape(8, 128, 128).transpose(1, 0, 2))
    oh = np.zeros((128, 8), np.float32)
    if core > 0:
        oh[:, core - 1] = 1.0
    c["onehot"] = oh
    return c


def host_shared(cfg, inp):
    import ml_dtypes
    bf = ml_dtypes.bfloat16
    D, CH = cfg["D"], cfg["CH"]
    CT = CH // 128
    sh = {}
    sh["ident"] = np.eye(128, dtype=np.float32).astype(bf)
    g = np.stack([inp["ffn1_norm"][0], inp["mix_norm"][0], inp["ffn2_norm"][0]]).astype(np.float32)
    sh["gcols"] = np.ascontiguousarray(g.reshape(3, D // 128, 128).transpose(2, 0, 1))
    kn = inp["k_norm"][0]
    sh["qkg"] = np.ascontiguousarray(np.stack([inp["q_norm"][0], kn[0], kn[1], kn[2]], axis=1).astype(np.float32))
    sh["convw"] = np.ascontiguousarray(inp["conv_w"][0].T.reshape(CT, 128, 31).transpose(1, 0, 2).astype(np.float32))
    cp = np.stack([inp["conv_b"][0], inp["conv_ln_g"][0], inp["conv_ln_b"][0]], axis=1)
    sh["convp"] = np.ascontiguousarray(cp.reshape(CT, 128, 3).transpose(1, 0, 2).astype(np.float32))
    sh["posT"] = np.ascontiguousarray(np.stack([inp["cmp_pos_k"][0].T, inp["cmp_pos_v"][0].T], axis=1).astype(np.float32))
    E2 = np.zeros((128, 64, 128), np.float32)
    for i in range(64):
        E2[2 * i, i, 0:64] = 1.0
        E2[2 * i + 1, i, 64:128] = 1.0
    sh["E2"] = E2.astype(bf)
    cj = np.arange(512)[:, None] * 16
    si = np.arange(128)[None, :] * 64
    ov = np.clip(np.minimum(cj + 32, si + 64) - np.maximum(cj, si), 0, None) / 32.0
    ov[511, :] = 0.0
    sh["Cov"] = np.ascontiguousarray(ov.reshape(4, 128, 128).transpose(1, 0, 2)).astype(bf)
    lm = np.zeros((128, 4, CHK), np.float32)
    for r in range(4):
        kk = r * 128 + np.arange(128)
        q = np.arange(CHK)
        ok = (kk[:, None] // 64 == q[None, :] // 64) & (kk[:, None] <= q[None, :])
        lm[:, r, :] = np.where(ok, 0.0, NEGB)
    sh["lmaskb"] = lm.astype(bf)
    return sh


def kernel(**inputs):
    return run_kernel(CFG_FULL, inputs, MODE)


def run_kernel(cfg, inputs, mode):
    inp = {k: np.asarray(v) for k, v in inputs.items()}
    D = cfg["D"]
    xs = inp["x"].astype(np.float32).reshape(NCORE, TOK, D)
    sh = host_shared(cfg, inp)
    per = [host_constants(cfg, c) for c in range(NCORE)]
    wA = dict(wg1=inp["ffn1_w_gate"][0], wu1=inp["ffn1_w_up"][0], wd1=inp["ffn1_w_down"][0], w_in=inp["w_in"][0])
    wB = dict(w_in=inp["w_in"][0], cmp_k_w1=inp["cmp_k_w1"][0], cmp_k_w2=inp["cmp_k_w2"][0],
              cmp_v_w1=inp["cmp_v_w1"][0], cmp_v_w2=inp["cmp_v_w2"][0], nsa_w_o=inp["nsa_w_o"][0],
              conv_w_o=inp["conv_w_o"][0], w_out=inp["w_out"][0],
              wg2=inp["ffn2_w_gate"][0], wu2=inp["ffn2_w_up"][0], wd2=inp["ffn2_w_down"][0])
    shA = {k: sh[k] for k in ("ident", "gcols", "qkg")}
    shB = dict(sh)
    if mode == "fused":
        nc = build(cfg, "all")
        ims = [dict(shB, **wA, **wB, **per[c], x=np.ascontiguousarray(xs[c])) for c in range(NCORE)]
        res = run_bass_kernel_spmd(nc, ims, core_ids=list(range(NCORE)))
        o = np.concatenate([r["out"] for r in res.results], axis=0)
        return o.reshape(1, SEQ, D).astype(np.float32)
    ncA = build(cfg, "A")
    imsA = [dict(shA, **wA, x=np.ascontiguousarray(xs[c])) for c in range(NCORE)]
    resA = run_bass_kernel_spmd(ncA, imsA, core_ids=list(range(NCORE)))
    gathered = np.concatenate([r["payload"] for r in resA.results], axis=0)
    ncB = build(cfg, "B")
    imsB = []
    for c in range(NCORE):
        r = resA.results[c]
        d = dict(shB, **wB, **per[c], gathered=gathered)
        for k in ("x1", "payload", "qT_s", "gates_s", "u_s"):
            d[k] = r[k]
        imsB.append(d)
    resB = run_bass_kernel_spmd(ncB, imsB, core_ids=list(range(NCORE)))
    if DEBUG:
        DBG["A"] = resA.results
        DBG["B"] = resB.results
    o = np.concatenate([r["out"] for r in resB.results], axis=0)
    return o.reshape(1, SEQ, D).astype(np.float32)
```

```python
import numpy as np
from contextlib import ExitStack
import concourse.bass as bass
import concourse.mybir as mybir
from concourse.bass_utils import run_bass_kernel_spmd

F32 = mybir.dt.float32
BF16 = mybir.dt.bfloat16
AF = mybir.ActivationFunctionType
ALU = mybir.AluOpType
AX = mybir.AxisListType

ENGS = ("pe", "act", "dve", "pool", "sp")
EPS = 1e-6
TOK = 1024
CHK = 512
NCORE = 8
SEQ = 8192
NEGB = -8192.0
SCALE = 128.0 ** -0.5
MODE = "split"
DEBUG = False
DBG = {}

CFG_FULL = dict(D=4096, FF=11008, KG=22, HALVES=[(0, 44), (44, 42)], FHMAX=44, CH=2048)


class Slot:
    def __init__(self, name, buf):
        self.name = name
        self.buf = buf
        self.cnt = 0
        self.free = []
        self.tok = None


class Prog:
    def __init__(self, nc, es):
        self.nc = nc
        self.es = es
        self.q = {e: [] for e in ENGS}
        self.cnt = {e: 0 for e in ENGS}
        self.waited = {e: {} for e in ENGS}
        self.slots = []
        self.nslot = 0
        self.esems = {e: es.enter_context(nc.semaphore("sem_" + e)) for e in ENGS}
        self.ssems = {}
        self.lastw = {}
        self.readers = {}
        self.nops = 0
        self.rings = {}
        self.ring_i = {}

    def slot(self, name, buf=None):
        self.nslot += 1
        s = Slot(f"{name}_{self.nslot}", buf)
        self.slots.append(s)
        return s

    def _waits(self, eng, deps):
        for d in deps:
            if d is None:
                continue
            k, v = d
            if self.waited[eng].get(k, 0) >= v:
                continue
            self.waited[eng][k] = v
            self.q[eng].append(("wait", k, v))

    def op(self, eng, fn, deps=(), sig=False):
        self._waits(eng, deps)
        tok = None
        if sig:
            self.cnt[eng] += 1
            tok = (eng, self.cnt[eng])
        self.q[eng].append(("op", fn, tok))
        self.nops += 1
        return tok

    def dma(self, eng, slot, fn, deps=()):
        self._waits(eng, deps)
        slot.cnt += 16
        tok = ("slot:" + slot.name, slot.cnt)
        self.q[eng].append(("dma", fn, slot))
        self.nops += 1
        return tok

    def _auto_deps(self, eng, reads, writes):
        deps = []
        for k in list(reads) + list(writes):
            w = self.lastw.get(k)
            if w is not None and not (eng == "pe" and w[0] == "pe"):
                deps.append(w)
        for k in writes:
            for r in self.readers.get(k, {}).values():
                if not (eng == "pe" and r[0] == "pe"):
                    deps.append(r)
        return deps

    def _auto_post(self, tok, reads, writes):
        for k in writes:
            self.lastw[k] = tok
            self.readers[k] = {}
        for k in reads:
            self.readers.setdefault(k, {})[tok[0]] = tok

    def aop(self, eng, fn, reads=(), writes=(), extra=()):
        deps = self._auto_deps(eng, reads, writes) + list(extra)
        tok = self.op(eng, fn, deps, sig=True)
        self._auto_post(tok, reads, writes)
        return tok

    def adma(self, eng, fn, reads=(), writes=(), extra=(), name="d"):
        ring = self.rings.setdefault(eng, [])
        nmax = 24 if eng == "sp" else 8
        i = self.ring_i.get(eng, 0)
        self.ring_i[eng] = i + 1
        if len(ring) < nmax:
            ring.append(self.slot("ring" + eng))
        s = ring[i % nmax]
        deps = self._auto_deps(eng, reads, writes) + list(extra)
        if s.cnt > 0:
            deps.append(("slot:" + s.name, s.cnt))
        tok = self.dma(eng, s, fn, deps)
        self._auto_post(tok, reads, writes)
        return tok

    def barrier(self):
        toks = [(e, self.cnt[e]) for e in ENGS if self.cnt[e] > 0]
        toks += [("slot:" + s.name, s.cnt) for s in self.slots if s.cnt > 0]
        for eng in ENGS:
            self._waits(eng, toks)
        self.lastw = {}
        self.readers = {}

    def flush(self, final_waits=None):
        nc = self.nc
        handles = {"pe": "tensor", "act": "scalar", "dve": "vector", "pool": "gpsimd", "sp": "sync"}
        final_waits = final_waits or {}

        def semof(k):
            if k.startswith("slot:"):
                n = k[5:]
                if n not in self.ssems:
                    self.ssems[n] = self.es.enter_context(nc.semaphore("ss_" + n))
                return self.ssems[n]
            return self.esems[k]

        def run(eng, e):
            for item in self.q[eng]:
                if item[0] == "wait":
                    e.wait_ge(semof(item[1]), item[2])
                elif item[0] == "op":
                    ins = item[1](e)
                    if item[2] is not None:
                        ins.then_inc(self.esems[eng], 1)
                else:
                    ins = item[1](e)
                    if item[2].name.startswith("cc"):
                        ins.then_inc(semof("slot:" + item[2].name))
                    else:
                        ins.then_inc(semof("slot:" + item[2].name), 16)
            for (k, v) in final_waits.get(eng, []):
                e.wait_ge(semof(k), v)

        with nc.Block() as block:
            for eng in ENGS:
                getattr(block, handles[eng])(lambda e, eng=eng: run(eng, e))
        self.q = {e: [] for e in ENGS}


class Ctx:
    pass


def MM(P, out, lhsT, rhs, start, stop, R, W):
    return P.aop("pe", lambda e: e.matmul(out, lhsT=lhsT, rhs=rhs, start=start, stop=stop), R, W)


def TR(P, out, in_, ident, R, W):
    return P.aop("pe", lambda e: e.transpose(out=out, in_=in_, identity=ident), R, W)


def ACT(P, out, in_, func, R, W, scale=None, bias=None, accum=None):
    kw = {}
    if scale is not None:
        kw["scale"] = scale
    if bias is not None:
        kw["bias"] = bias
    if accum is not None:
        kw["accum_out"] = accum
    return P.aop("act", lambda e: e.activation(out=out, in_=in_, func=func, **kw), R, W)


def TS(P, eng, out, in0, s1, s2, op0, op1, R, W):
    if op1 is None:
        return P.aop(eng, lambda e: e.tensor_scalar(out=out, in0=in0, scalar1=s1, scalar2=None, op0=op0), R, W)
    return P.aop(eng, lambda e: e.tensor_scalar(out=out, in0=in0, scalar1=s1, scalar2=s2, op0=op0, op1=op1), R, W)


def TT(P, eng, out, in0, in1, op, R, W):
    return P.aop(eng, lambda e: e.tensor_tensor(out=out, in0=in0, in1=in1, op=op), R, W)


def STT(P, eng, out, in0, scalar, in1, op0, op1, R, W):
    return P.aop(eng, lambda e: e.scalar_tensor_tensor(out=out, in0=in0, scalar=scalar, in1=in1, op0=op0, op1=op1), R, W)


def DMA(P, eng, out, in_, R, W, name="d"):
    return P.adma(eng, lambda e: e.dma_start(out=out, in_=in_), R, W, name=name)


def setup_common(nc, P, es, cfg, tag, nact, nxs=2, with_sg=True, wslot=8192):
    D = cfg["D"]
    KT = D // 128
    C = Ctx()
    C.cfg = cfg
    sb = lambda name, shape, dt: es.enter_context(nc.sbuf_tensor(f"{tag}_{name}", shape, dt))
    C.sb = sb
    C.hT = sb("hT", [128, KT, CHK], BF16)
    C.act = sb("act", [128, nact, CHK], BF16)
    C.wslot = wslot
    C.wreg = sb("wreg", [128, 4 * wslot], BF16)
    C.xs = [P.slot(f"xs{i}", sb(f"xs{i}", [128, D], F32)) for i in range(nxs)]
    C.hb = sb("hb", [128, D], BF16)
    C.ss = sb("ss", [128, 8], F32)
    C.sg = [sb(f"sg{i}", [128, CHK], F32) for i in range(2)] if with_sg else None
    C.xres = [P.slot(f"xres{i}", sb(f"xres{i}", [128, 512], F32)) for i in range(4)]
    C.ost = [P.slot(f"ost{i}", sb(f"ost{i}", [128, 512], F32)) for i in range(4)]
    C.ident = sb("ident_sb", [128, 128], BF16)
    C.gcols = sb("gcols", [128, 3, KT], F32)
    C.cslot = P.slot("consts", None)
    C.wgu = [P.slot(f"wgu{i}", C.wreg[:, i * wslot:(i + 1) * wslot]) for i in range(4)]
    KG = cfg["KG"]
    C.wd = [P.slot(f"wd{i}", C.wreg[:, i * KG * 512:(i + 1) * KG * 512]) for i in range(2)]
    ps = lambda name, shape, dt: es.enter_context(nc.psum_tensor(f"{tag}_{name}", shape, dt))
    C.ps = [ps(f"ps{i}", [128, 512], F32) for i in range(6)]
    C.pT = [ps(f"pT{i}", [128, 1024], BF16) for i in range(2)]
    C.ps_free = [None] * 6
    C.pT_free = [None] * 2
    C.hT_free = None
    C.act_free = None
    C.hb_free = None
    C.sg_free = [None, None]
    C.wreg_free = None
    C.nss = 0
    C.wu_i = 0
    C.wd_i = 0
    return C


def load_consts(P, C, ident_d, gn_d):
    P.dma("sp", C.cslot, lambda e: e.dma_start(out=C.ident[:], in_=ident_d[:, :]))
    return P.dma("sp", C.cslot, lambda e: e.dma_start(out=C.gcols[:], in_=gn_d[:, :, :]))


def norm_transpose(P, C, src, r0, gi, consts_tok):
    D = C.cfg["D"]
    KT = D // 128
    last = None
    ng = (KT + 7) // 8
    for m in range(4):
        sl = C.xs[m % len(C.xs)]
        rows = src[r0 + m * 128:r0 + (m + 1) * 128, :]
        t_x = P.dma("sp", sl, lambda e, sl=sl, rows=rows: e.dma_start(out=sl.buf[:], in_=rows), deps=sl.free)
        col = C.nss % 8
        C.nss += 1
        sc = C.ss[:, col:col + 1]
        t_sq = P.op("act", lambda e, sl=sl, sc=sc: e.activation(out=C.hb[:], in_=sl.buf[:], func=AF.Square, accum_out=sc),
                    deps=[t_x, C.hb_free], sig=True)
        t1 = P.op("dve", lambda e, sc=sc: e.tensor_scalar(out=sc, in0=sc, scalar1=1.0 / D, scalar2=EPS,
                                                         op0=ALU.mult, op1=ALU.add), deps=[t_sq], sig=True)
        t2 = P.op("act", lambda e, sc=sc: e.activation(out=sc, in_=sc, func=AF.Sqrt), deps=[t1], sig=True)
        t3 = P.op("dve", lambda e, sc=sc: e.reciprocal(out=sc, in_=sc), deps=[t2], sig=True)
        t_hb = P.op("act", lambda e, sl=sl, sc=sc: e.activation(out=C.hb[:], in_=sl.buf[:], func=AF.Copy, scale=sc),
                    deps=[t3], sig=True)
        sl.free = [t_hb]
        t_tr = None
        for g0 in range(0, KT, 8):
            gi_ = (g0 // 8 + m * ng) % 2
            pT = C.pT[gi_]
            n = min(8, KT - g0)
            for kk in range(n):
                kt = g0 + kk
                o_ap = pT[:, kk * 128:(kk + 1) * 128]
                i_ap = C.hb[:, kt * 128:(kt + 1) * 128]
                t_tr = P.op("pe", lambda e, o_ap=o_ap, i_ap=i_ap: e.transpose(out=o_ap, in_=i_ap, identity=C.ident[:]),
                            deps=[t_hb, C.pT_free[gi_], consts_tok] if kk == 0 else [], sig=(kk == n - 1))
            t_ev = None
            for kk in range(n):
                kt = g0 + kk
                o_ap = C.hT[:, kt, m * 128:(m + 1) * 128]
                i_ap = pT[:, kk * 128:(kk + 1) * 128]
                g_ap = C.gcols[:, gi, kt:kt + 1]
                t_ev = P.op("dve", lambda e, o_ap=o_ap, i_ap=i_ap, g_ap=g_ap: e.tensor_scalar(
                    out=o_ap, in0=i_ap, scalar1=g_ap, scalar2=None, op0=ALU.mult),
                    deps=[t_tr, C.hT_free] if kk == 0 else [], sig=(kk == n - 1))
            C.pT_free[gi_] = t_ev
            last = t_ev
        C.hb_free = t_tr
    return last


def down_proj(P, C, actbuf, ts, nt, Wdv, scale, src, dst, r0, t_act_ready, gu_last, store_toks):
    cfg = C.cfg
    D, KG = cfg["D"], cfg["KG"]
    NCH = D // 512
    groups = [(g0, min(KG, nt - g0)) for g0 in range(0, nt, KG)]
    t_pe = None
    for n in range(NCH):
        t_po = [None] * 4
        for gidx, (g0, gn) in enumerate(groups):
            ws = C.wd[C.wd_i % 2]
            C.wd_i += 1
            o_ap = ws.buf[:, 0:gn * 512].rearrange("p (k n) -> p k n", n=512)
            i_ap = Wdv[:, ts + g0:ts + g0 + gn, n * 512:(n + 1) * 512]
            t_wd = P.dma("pool", ws, lambda e, o_ap=o_ap, i_ap=i_ap: e.dma_start(out=o_ap, in_=i_ap),
                         deps=ws.free + [gu_last, C.wreg_free])
            wv = o_ap
            for m in range(4):
                po = C.ps[m]
                for kk in range(gn):
                    first = (gidx == 0 and kk == 0)
                    lastk = (gidx == len(groups) - 1 and kk == gn - 1)
                    l_ap = actbuf[:, g0 + kk, m * 128:(m + 1) * 128]
                    r_ap = wv[:, kk, :]
                    t_pe = P.op("pe", lambda e, po=po, l_ap=l_ap, r_ap=r_ap, first=first, lastk=lastk: e.matmul(
                        po[:], lhsT=l_ap, rhs=r_ap, start=first, stop=lastk),
                        deps=([t_wd, t_act_ready] + ([C.ps_free[m]] if first else [])) if kk == 0 else [],
                        sig=(lastk or kk == gn - 1))
                if gidx == len(groups) - 1:
                    t_po[m] = t_pe
            ws.free = [t_pe]
        for m in range(4):
            xr, os_ = C.xres[m], C.ost[m]
            s_ap = src[r0 + m * 128:r0 + (m + 1) * 128, n * 512:(n + 1) * 512]
            d_ap = dst[r0 + m * 128:r0 + (m + 1) * 128, n * 512:(n + 1) * 512]
            key = (r0, m, n)
            t_xr = P.dma("sp", xr, lambda e, xr=xr, s_ap=s_ap: e.dma_start(out=xr.buf[:], in_=s_ap),
                         deps=xr.free + [store_toks.get((id(src), key))])
            t_o = P.op("dve", lambda e, po=C.ps[m], xr=xr, os_=os_: e.scalar_tensor_tensor(
                out=os_.buf[:], in0=po[:], scalar=scale, in1=xr.buf[:], op0=ALU.mult, op1=ALU.add),
                deps=[t_po[m], t_xr] + os_.free, sig=True)
            C.ps_free[m] = t_o
            xr.free = [t_o]
            t_st = P.dma("sp", os_, lambda e, os_=os_, d_ap=d_ap: e.dma_start(out=d_ap, in_=os_.buf[:]), deps=[t_o])
            os_.free = [t_st]
            store_toks[(id(dst), key)] = t_st
    return t_pe


def ffn(P, C, xsrc, xtmp, xdst, gi, Wg, Wu, Wd, consts_tok, store_toks):
    cfg = C.cfg
    D = cfg["D"]
    KT = D // 128
    halves = cfg["HALVES"]
    Wgv = Wg.rearrange("(kt p) n -> p kt n", p=128)
    Wuv = Wu.rearrange("(kt p) n -> p kt n", p=128)
    Wdv = Wd.rearrange("(ft p) n -> p ft n", p=128)
    bi = 0
    for c in range(2):
        r0 = c * CHK
        t_hT = norm_transpose(P, C, xsrc, r0, gi, consts_tok)
        t_last_pe = None
        for hf, (ts, nt) in enumerate(halves):
            src = xsrc if hf == 0 else xtmp
            dst = xdst if hf == len(halves) - 1 else xtmp
            t_act_last = None
            for u in range(nt // 2):
                sg_, su_ = C.wgu[(C.wu_i % 2) * 2], C.wgu[(C.wu_i % 2) * 2 + 1]
                C.wu_i += 1
                c0 = (ts + 2 * u) * 128
                gv = sg_.buf[:, 0:KT * 256].rearrange("p (k n) -> p k n", n=256)
                uv = su_.buf[:, 0:KT * 256].rearrange("p (k n) -> p k n", n=256)
                gi_ap = Wgv[:, :, c0:c0 + 256]
                ui_ap = Wuv[:, :, c0:c0 + 256]
                t_wg = P.dma("pool", sg_, lambda e, gv=gv, gi_ap=gi_ap: e.dma_start(out=gv, in_=gi_ap),
                             deps=sg_.free + [C.wreg_free])
                t_wu = P.dma("pool", su_, lambda e, uv=uv, ui_ap=ui_ap: e.dma_start(out=uv, in_=ui_ap),
                             deps=su_.free + [C.wreg_free])
                for jj in range(2):
                    jl = 2 * u + jj
                    ig, iu = bi * 2, bi * 2 + 1
                    pg, pu = C.ps[ig], C.ps[iu]
                    sgb = C.sg[bi]
                    for kt in range(KT):
                        l_ap = gv[:, kt, jj * 128:(jj + 1) * 128]
                        r_ap = C.hT[:, kt, :]
                        t_g = P.op("pe", lambda e, pg=pg, l_ap=l_ap, r_ap=r_ap, kt=kt: e.matmul(
                            pg[:], lhsT=l_ap, rhs=r_ap, start=(kt == 0), stop=(kt == KT - 1)),
                            deps=[t_wg, t_hT, C.ps_free[ig]] if kt == 0 else [], sig=(kt == KT - 1))
                    for kt in range(KT):
                        l_ap = uv[:, kt, jj * 128:(jj + 1) * 128]
                        r_ap = C.hT[:, kt, :]
                        t_u = P.op("pe", lambda e, pu=pu, l_ap=l_ap, r_ap=r_ap, kt=kt: e.matmul(
                            pu[:], lhsT=l_ap, rhs=r_ap, start=(kt == 0), stop=(kt == KT - 1)),
                            deps=[t_wu, C.ps_free[iu]] if kt == 0 else [], sig=(kt == KT - 1))
                    t_sg = P.op("act", lambda e, pg=pg, sgb=sgb: e.activation(out=sgb[:], in_=pg[:], func=AF.Silu),
                                deps=[t_g, C.sg_free[bi]], sig=True)
                    C.ps_free[ig] = t_sg
                    a_ap = C.act[:, jl, :]
                    t_a = P.op("dve", lambda e, pu=pu, sgb=sgb, a_ap=a_ap: e.tensor_tensor(
                        out=a_ap, in0=sgb[:], in1=pu[:], op=ALU.mult),
                        deps=[t_sg, t_u, C.act_free], sig=True)
                    C.ps_free[iu] = t_a
                    C.sg_free[bi] = t_a
                    t_act_last = t_a
                    t_last_pe = t_u
                    bi ^= 1
                sg_.free = [t_last_pe]
                su_.free = [t_last_pe]
            t_pe = down_proj(P, C, C.act, ts, nt, Wdv, 0.5, src, dst, r0, t_act_last, t_last_pe, store_toks)
            C.act_free = t_pe
            C.wreg_free = t_pe
        C.hT_free = t_last_pe


def offsets(cfg):
    CH, D = cfg["CH"], cfg["D"]
    o = {}
    o["q"] = 0
    o["kc"] = 2048
    o["vc"] = 2560
    o["ks"] = 3072
    o["vs"] = 3584
    o["kw"] = 4096
    o["vw"] = 4608
    o["gn"] = 5120
    o["ua"] = 5168
    o["ug"] = 5168 + CH
    o["ga"] = 5168 + 2 * CH
    o["gb"] = 5168 + 2 * CH + D
    o["nin"] = 5168 + 2 * CH + 2 * D
    return o


def payload_rows(cfg):
    return 3072 + cfg["CH"] // 32


def payload_views(pl, cfg):
    CH = cfg["CH"]
    v = {}
    v["ksT"] = pl[0:512, :]
    v["kwT"] = pl[512:1024, :]
    v["kcT"] = pl[1024:1536, :]
    v["vcT"] = pl[1536:2048, :]
    v["vs"] = pl[2048:2560, :].rearrange("a (b c) -> (a b) c", b=2)
    v["vw"] = pl[2560:3072, :].rearrange("a (b c) -> (a b) c", b=2)
    v["uh"] = pl[3072:3072 + CH // 32, :].rearrange("a (b c) -> (a b) c", c=32)
    return v


def load_w256(P, C, Wv, c0, ncols, key):
    KT = C.cfg["D"] // 128
    i = C.wu_i % 4
    C.wu_i += 1
    k = f"wslot{i}"
    view = C.wreg[:, i * C.wslot:i * C.wslot + KT * ncols].rearrange("p (k n) -> p k n", n=ncols)
    DMA(P, "pool", view, Wv[:, :, c0:c0 + ncols], [], [k], name="w")
    return view, k


def proj_fm(P, C, wv, wk, jj, bank, bk):
    KT = C.cfg["D"] // 128
    for kt in range(KT):
        MM(P, bank[:], wv[:, kt, jj * 128:(jj + 1) * 128], C.hT[:, kt, :], kt == 0, kt == KT - 1, [wk, "hT"], [bk])


def proj_tm(P, C, wv, wk, ncols, m, bank, bk):
    KT = C.cfg["D"] // 128
    for kt in range(KT):
        MM(P, bank[:, 0:ncols], C.hT[:, kt, m * 128:(m + 1) * 128], wv[:, kt, 0:ncols], kt == 0, kt == KT - 1,
           [wk, "hT"], [bk])


def head_norm_T(P, C, M, bank, bk, nh, gcol_idx, dst_of_head):
    M.par ^= 1
    p = M.par
    sq, hs, qn, pT = M.sq2[p], M.hs2[p], M.qn2[p], C.pT[p]
    ksq, khs, kqn, kpT = f"sq{p}", f"hs{p}", f"qn{p}", f"pT{p}"
    ACT(P, sq[:, 0:nh * 128], bank[:, 0:nh * 128], AF.Square, [bk], [ksq])
    P.aop("dve", lambda e: e.tensor_reduce(out=hs[:, 0:nh], in_=sq[:, 0:nh * 128].rearrange("p (h d) -> p h d", d=128),
                                           axis=AX.X, op=ALU.add), [ksq], [khs])
    TS(P, "dve", hs[:, 0:nh], hs[:, 0:nh], 1.0 / 128, EPS, ALU.mult, ALU.add, [], [khs])
    ACT(P, hs[:, 0:nh], hs[:, 0:nh], AF.Sqrt, [], [khs])
    P.aop("dve", lambda e: e.reciprocal(out=hs[:, 0:nh], in_=hs[:, 0:nh]), [], [khs])
    for h in range(nh):
        ACT(P, qn[:, h * 128:(h + 1) * 128], bank[:, h * 128:(h + 1) * 128], AF.Copy, [bk, khs], [kqn],
            scale=hs[:, h:h + 1])
    for h in range(nh):
        TR(P, pT[:, h * 128:(h + 1) * 128], qn[:, h * 128:(h + 1) * 128], C.ident[:], [kqn], [kpT])
    for h in range(nh):
        TS(P, "dve", dst_of_head(h), pT[:, h * 128:(h + 1) * 128], M.qkg[:, gcol_idx:gcol_idx + 1], None,
           ALU.mult, None, [kpT, "qkg"], [M.stgkey])


def mixer_proj(P, nc, es, cfg, io):
    D, CH = cfg["D"], cfg["CH"]
    CT = CH // 128
    off = offsets(cfg)
    with ExitStack() as es2:
        C = setup_common(nc, P, es2, cfg, "m1", 1, nxs=2, with_sg=False)
        M = Ctx()
        sb = C.sb
        M.sq2 = [sb(f"sq{i}", [128, 256], F32) for i in range(2)]
        M.hs2 = [sb(f"hs{i}", [128, 2], F32) for i in range(2)]
        M.qn2 = [sb(f"qn{i}", [128, 256], BF16) for i in range(2)]
        M.par = 0
        M.stgkey = "stg"
        M.stgT2 = [sb(f"stgT{i}", [128, 2, 512], BF16) for i in range(2)]
        M.sgi = 0
        M.qkg = sb("qkg", [128, 4], F32)
        M.stgT = sb("stgT", [128, 2, 512], BF16)
        M.stgV = sb("stgV", [128, 256], BF16)
        M.stgG = sb("stgG", [128, 48], F32)
        M.sig = sb("sig", [128, 512], F32)
        consts_tok = load_consts(P, C, io["ident"], io["gcols"])
        DMA(P, "sp", M.qkg[:], io["qkg"][:, :], [], ["qkg"])
        Wv = io["w_in"].rearrange("(kt p) n -> p kt n", p=128)
        pv = payload_views(io["payload"], cfg)
        banks = [(C.ps[i], f"ps{i}") for i in range(6)]
        bi = 0

        def nb():
            nonlocal bi
            b = banks[bi % 6]
            bi += 1
            return b

        for c in range(2):
            r0 = c * CHK
            P.barrier()
            C.hT_free = None
            t_hT = norm_transpose(P, C, io["x1"], r0, 1, consts_tok)
            P.lastw["hT"] = t_hT
            P.barrier()
            for (name, gcol, ngroups) in (("q", 0, 8), ("ks", 2, 2), ("kw", 3, 2)):
                for g in range(ngroups):
                    wv, wk = load_w256(P, C, Wv, off[name] + g * 256, 256, None)
                    M.sgi ^= 1
                    stg = M.stgT2[M.sgi]
                    M.stgkey = f"stgA{M.sgi}"
                    for m in range(4):
                        bank, bk = nb()
                        proj_tm(P, C, wv, wk, 256, m, bank, bk)
                        head_norm_T(P, C, M, bank, bk, 2, gcol, lambda h, m=m, stg=stg: stg[:, h, m * 128:(m + 1) * 128])
                    for h in range(2):
                        hh = g * 2 + h
                        if name == "q":
                            dst = io["qT_s"][hh, :, r0:r0 + CHK]
                        elif name == "ks":
                            dst = pv["ksT"][hh * 128:(hh + 1) * 128, r0:r0 + CHK]
                        else:
                            dst = pv["kwT"][hh * 128:(hh + 1) * 128, r0:r0 + CHK]
                        DMA(P, "sp", dst, stg[:, h, :], [M.stgkey], [], name="st")
            for name in ("kc", "vc"):
                for g in range(2):
                    wv, wk = load_w256(P, C, Wv, off[name] + g * 256, 256, None)
                    for jj in range(2):
                        bank, bk = nb()
                        proj_fm(P, C, wv, wk, jj, bank, bk)
                        ACT(P, M.stgT[:, jj, :], bank[:], AF.Copy, [bk], ["stg"])
                        hh = g * 2 + jj
                        dst = pv[name + "T"][hh * 128:(hh + 1) * 128, r0:r0 + CHK]
                        DMA(P, "sp", dst, M.stgT[:, jj, :], ["stg"], [], name="st")
            for name in ("vs", "vw"):
                for g in range(2):
                    wv, wk = load_w256(P, C, Wv, off[name] + g * 256, 256, None)
                    for m in range(4):
                        bank, bk = nb()
                        proj_tm(P, C, wv, wk, 256, m, bank, bk)
                        ACT(P, M.stgV[:], bank[:, 0:256], AF.Copy, [bk], ["stgV"])
                        dst = pv[name][r0 + m * 128:r0 + (m + 1) * 128, g * 256:(g + 1) * 256]
                        DMA(P, "sp", dst, M.stgV[:], ["stgV"], [], name="st")
            wv, wk = load_w256(P, C, Wv, off["gn"], 48, None)
            for m in range(4):
                bank, bk = nb()
                proj_tm(P, C, wv, wk, 48, m, bank, bk)
                ACT(P, M.stgG[:], bank[:, 0:48], AF.Sigmoid, [bk], ["stgG"])
                DMA(P, "sp", io["gates_s"][r0 + m * 128:r0 + (m + 1) * 128, :], M.stgG[:], ["stgG"], [], name="st")
            for ct in range(CT):
                wa, wak = load_w256(P, C, Wv, off["ua"] + ct * 128, 128, None)
                wg, wgk = load_w256(P, C, Wv, off["ug"] + ct * 128, 128, None)
                ba, bak = nb()
                proj_fm(P, C, wa, wak, 0, ba, bak)
                bg, bgk = nb()
                proj_fm(P, C, wg, wgk, 0, bg, bgk)
                ACT(P, M.sig[:], bg[:], AF.Sigmoid, [bgk], ["sig"])
                TT(P, "dve", M.stgT[:, 0, :], M.sig[:], ba[:], ALU.mult, ["sig", bak], ["stg"])
                DMA(P, "sp", io["u_s"][ct * 128:(ct + 1) * 128, r0:r0 + CHK], M.stgT[:, 0, :], ["stg"], [], name="st")
                if c == 1:
                    DMA(P, "sp", pv["uh"][ct * 128:(ct + 1) * 128, :], M.stgT[:, 0, CHK - 32:CHK], ["stg"], [], name="st")
        P.barrier()
        P.flush()


def mixer_conv(P, nc, es, cfg, io):
    CH = cfg["CH"]
    CT = CH // 128
    with ExitStack() as es2:
        sb = lambda name, shape, dt: es2.enter_context(nc.sbuf_tensor("cv_" + name, shape, dt))
        ps = lambda name, shape, dt: es2.enter_context(nc.psum_tensor("cv_" + name, shape, dt))
        ue = sb("ue", [128, CT, 32 + TOK], BF16)
        uh = sb("uh", [128, CT, 8, 32], BF16)
        onehot = sb("onehot", [128, 8], F32)
        cw = sb("cw", [128, CT, 31], F32)
        cp = sb("cp", [128, CT, 3], F32)
        cbuf = sb("cbuf", [128, CT, CHK], F32)
        sqb = sb("sqb", [128, CHK], F32)
        mu = sb("mu", [128, CHK], F32)
        rs = sb("rs", [128, CHK], F32)
        t1 = sb("t1", [128, CHK], F32)
        ones = sb("ones", [128, 128], F32)
        stg = sb("stg", [128, CHK], BF16)
        p1 = ps("p1", [128, 512], F32)
        p2 = ps("p2", [128, 512], F32)
        DMA(P, "sp", onehot[:], io["onehot"][:, :], [], ["onehot"])
        DMA(P, "sp", cw[:], io["convw"][:, :, :], [], ["cw"])
        DMA(P, "sp", cp[:], io["convp"][:, :, :], [], ["cp"])
        P.aop("pool", lambda e: e.memset(ones[:], 1.0), [], ["ones"])
        g3 = io["gathered"].rearrange("(r a) c -> r a c", r=NCORE)
        R0 = 3072
        for ct in range(CT):
            DMA(P, "sp", ue[:, ct, 32:32 + TOK], io["u_s"][ct * 128:(ct + 1) * 128, :], [], ["ue"])
            src = g3[:, R0 + ct * 4:R0 + ct * 4 + 4, :].rearrange("r a (b c) -> (a b) r c", c=32)
            DMA(P, "sp", uh[:, ct, :, :], src, [], ["uh"])
        for ct in range(CT):
            TS(P, "dve", ue[:, ct, 0:32], uh[:, ct, 0, :], onehot[:, 0:1], None, ALU.mult, None, ["uh", "onehot"], ["ue"])
            for r in range(1, 8):
                STT(P, "dve", ue[:, ct, 0:32], uh[:, ct, r, :], onehot[:, r:r + 1], ue[:, ct, 0:32], ALU.mult, ALU.add,
                    ["uh", "onehot"], ["ue"])
        for c in range(2):
            c0 = c * CHK
            IL = 4 if CT % 4 == 0 else 2
            for cg in range(0, CT, IL):
                for k in range(31):
                    for ct in range(cg, cg + IL):
                        key = f"cbuf{ct}"
                        if k == 0:
                            TS(P, "dve", cbuf[:, ct, :], ue[:, ct, c0 + 2:c0 + 2 + CHK], cw[:, ct, 0:1], cp[:, ct, 0:1],
                               ALU.mult, ALU.add, ["ue", "cw", "cp"], [key])
                        else:
                            STT(P, "dve", cbuf[:, ct, :], ue[:, ct, c0 + 2 + k:c0 + 2 + k + CHK], cw[:, ct, k:k + 1],
                                cbuf[:, ct, :], ALU.mult, ALU.add, ["ue", "cw"], [key])
            for ct in range(CT):
                MM(P, p1[:], ones[:], cbuf[:, ct, :], ct == 0, ct == CT - 1, ["ones", f"cbuf{ct}"], ["p1"])
            for ct in range(CT):
                ACT(P, sqb[:], cbuf[:, ct, :], AF.Square, [f"cbuf{ct}"], ["sqb"])
                MM(P, p2[:], ones[:], sqb[:], ct == 0, ct == CT - 1, ["ones", "sqb"], ["p2"])
            TS(P, "dve", mu[:], p1[:], 1.0 / CH, None, ALU.mult, None, ["p1"], ["mu"])
            TT(P, "dve", t1[:], mu[:], mu[:], ALU.mult, ["mu"], ["t1"])
            STT(P, "dve", rs[:], p2[:], 1.0 / CH, t1[:], ALU.mult, ALU.subtract, ["p2", "t1"], ["rs"])
            TS(P, "dve", rs[:], rs[:], EPS, None, ALU.add, None, [], ["rs"])
            ACT(P, rs[:], rs[:], AF.Sqrt, [], ["rs"])
            P.aop("dve", lambda e: e.reciprocal(out=rs[:], in_=rs[:]), [], ["rs"])
            for ct in range(CT):
                TT(P, "dve", t1[:], cbuf[:, ct, :], mu[:], ALU.subtract, [f"cbuf{ct}", "mu"], ["t1"])
                TT(P, "dve", t1[:], t1[:], rs[:], ALU.mult, ["rs"], ["t1"])
                ACT(P, stg[:], t1[:], AF.Silu, ["t1", "cp"], ["stg"], scale=cp[:, ct, 1:2], bias=cp[:, ct, 2:3])
                DMA(P, "sp", io["convT_s"][ct * 128:(ct + 1) * 128, c0:c0 + CHK], stg[:], ["stg"], [], name="st")
        P.barrier()
        P.flush()


def mixer_compress(P, nc, es, cfg, io, K):
    with ExitStack() as es2:
        sb = lambda name, shape, dt: es2.enter_context(nc.sbuf_tensor("cp_" + name, shape, dt))
        ps = lambda name, shape, dt: es2.enter_context(nc.psum_tensor("cp_" + name, shape, dt))
        kbig = sb("kbig", [128, SEQ + 16], BF16)
        W1 = sb("W1", [128, 32, 512], BF16)
        W2 = sb("W2", [128, 4, 128], BF16)
        pos = sb("pos", [128, 2, 32], F32)
        qkg = sb("qkg", [128, 4], F32)
        ident = sb("ident", [128, 128], BF16)
        flat = [sb(f"flat{i}", [128, 512], BF16) for i in range(2)]
        hid = sb("hid", [128, 4, 512], BF16)
        x2 = sb("x2", [128, 512], F32)
        tt = sb("tt", [128, 512], F32)
        sg = sb("sg", [128, 512], F32)
        tok = sb("tok", [128, 128], F32)
        junk = sb("junk", [128, 128], F32)
        tokb = sb("tokb", [128, 128], BF16)
        ssq = sb("ssq", [128, 1], F32)
        ph = [ps(f"ph{i}", [128, 512], F32) for i in range(4)]
        po = ps("po", [128, 512], F32)
        pT = ps("pT", [128, 1024], BF16)
        DMA(P, "sp", pos[:], io["posT"][:, :, :], [], ["pos"])
        DMA(P, "sp", qkg[:], io["qkg"][:, :], [], ["qkg"])
        DMA(P, "sp", ident[:], io["ident"][:, :], [], ["ident"])
        P.aop("pool", lambda e: e.memset(kbig[:, SEQ:SEQ + 16], 0.0), [], ["kbig"])
        P.aop("pool", lambda e: e.memset(K.vcmp[:, :, :, 128:129], 1.0), [], ["vcmp"])
        g3 = io["gathered"].rearrange("(r a) c -> r a c", r=NCORE)
        for which, (w1, w2, rbase) in enumerate(((io["cmp_k_w1"], io["cmp_k_w2"], 1024), (io["cmp_v_w1"], io["cmp_v_w2"], 1536))):
            w1v = w1.rearrange("(l p) n -> p l n", p=128)
            for l0 in range(0, 32, 8):
                DMA(P, "pool", W1[:, l0:l0 + 8, :], w1v[:, l0:l0 + 8, :], [], ["W1"], name="w")
            DMA(P, "pool", W2[:], w2.rearrange("(t p) n -> p t n", p=128), [], ["W2"], name="w")
            for hk in range(4):
                for r in range(NCORE):
                    DMA(P, "sp", kbig[:, r * TOK:(r + 1) * TOK], g3[r, rbase + hk * 128:rbase + (hk + 1) * 128, :], [], ["kbig"])
                kv = kbig[:, 0:SEQ + 16].rearrange("p (j s) -> p j s", s=16)
                for l in range(32):
                    fb = flat[l % 2]
                    fk = f"flat{l % 2}"
                    src = kv[:, (l // 16):(l // 16) + 512, l % 16]
                    TS(P, "dve" if l % 2 == 0 else "pool", fb[:], src, pos[:, which, l:l + 1], None, ALU.add, None,
                       ["kbig", "pos"], [fk])
                    for ht in range(4):
                        MM(P, ph[ht][:], W1[:, l, ht * 128:(ht + 1) * 128], fb[:], l == 0, l == 31, ["W1", fk], [f"ph{ht}"])
                for ht in range(4):
                    ACT(P, x2[:], ph[ht][:], AF.Square, [f"ph{ht}"], ["x2"])
                    TS(P, "dve", tt[:], x2[:], 0.044715, 1.0, ALU.mult, ALU.add, ["x2"], ["tt"])
                    TT(P, "dve", tt[:], tt[:], ph[ht][:], ALU.mult, [f"ph{ht}"], ["tt"])
                    ACT(P, sg[:], tt[:], AF.Sigmoid, ["tt"], ["sg"], scale=1.5957691216057308)
                    TT(P, "dve", hid[:, ht, :], sg[:], ph[ht][:], ALU.mult, ["sg", f"ph{ht}"], ["hid"])
                for jt in range(4):
                    for ht in range(4):
                        MM(P, po[:, 0:128], hid[:, ht, jt * 128:(jt + 1) * 128], W2[:, ht, :], ht == 0, ht == 3,
                           ["hid", "W2"], ["po"])
                    if which == 1:
                        ACT(P, K.vcmp[:, hk, jt, 0:128], po[:, 0:128], AF.Copy, ["po"], ["vcmp"])
                    else:
                        ACT(P, junk[:], po[:, 0:128], AF.Square, ["po"], ["junk", "ssq"], accum=ssq[:, 0:1])
                        TS(P, "dve", ssq[:], ssq[:], 1.0 / 128, EPS, ALU.mult, ALU.add, [], ["ssq"])
                        ACT(P, ssq[:], ssq[:], AF.Sqrt, [], ["ssq"])
                        P.aop("dve", lambda e: e.reciprocal(out=ssq[:], in_=ssq[:]), [], ["ssq"])
                        ACT(P, tokb[:], po[:, 0:128], AF.Copy, ["po", "ssq"], ["tokb"], scale=ssq[:, 0:1])
                        TR(P, pT[:, 0:128], tokb[:], ident[:], ["tokb", "ident"], ["pT"])
                        TS(P, "dve", K.kcmpT[:, hk, jt * 128:(jt + 1) * 128], pT[:, 0:128], qkg[:, 1:2], None, ALU.mult, None,
                           ["pT", "qkg"], ["kcmpT"])
        P.barrier()
        P.flush()


def mixer_attn(P, nc, es, cfg, io, K):
    with ExitStack() as es2:
        sb = lambda name, shape, dt: es2.enter_context(nc.sbuf_tensor("at_" + name, shape, dt))
        ps = lambda name, shape, dt: es2.enter_context(nc.psum_tensor("at_" + name, shape, dt))
        ident = sb("ident", [128, 128], BF16)
        E2 = sb("E2", [128, 64, 128], BF16)
        Cov = sb("Cov", [128, 4, 128], BF16)
        cmaskb = sb("cmaskb", [128, 4, TOK], BF16)
        wmaskb = sb("wmaskb", [128, 16, CHK], BF16)
        lmaskb = sb("lmaskb", [128, 4, CHK], BF16)
        selmul = sb("selmul", [128, 8, 128], F32)
        seladd = sb("seladd", [128, 8, 128], F32)
        farmask = sb("farmask", [128, 8, 128], F32)
        onehot = sb("onehot", [128, 8], F32)
        gates = sb("gates", [128, 8, 48], F32)
        kbig = sb("kbig", [128, SEQ], BF16)
        vbig = sb("vbig", [128, 64, 129], BF16)
        kwh = sb("kwh", [128, 8, 512], BF16)
        vwh = sb("vwh", [128, 8, 4, 128], BF16)
        kwe = sb("kwe", [128, 1536], BF16)
        vwe = sb("vwe", [128, 12, 129], BF16)
        qTh = [sb(f"qTh{i}", [128, TOK], BF16) for i in range(2)]
        pTs = [sb(f"pTs{i}", [128, CHK], BF16) for i in range(6)]
        osb = sb("osb", [128, 3, 4, 129], F32)
        rden = sb("rden", [128, 3, 4], F32)
        gsc = sb("gsc", [128, 3, 4], F32)
        impa = sb("impa", [128, 4, 128], F32)
        score = sb("score", [128, 128], F32)
        sc2 = sb("sc2", [128, 128], F32)
        m8a = sb("m8a", [128, 8], F32)
        m8b = sb("m8b", [128, 8], F32)
        selb = sb("selb", [128, 128], BF16)
        selTb = sb("selTb", [128, CHK], BF16)
        atm = sb("atm", [128, 128], F32)
        atb = sb("atb", [128, 128], BF16)
        aTs = sb("aTs", [128, CHK], BF16)
        pS = [ps(f"pS{i}", [128, 512], F32) for i in range(2)]
        pA = [ps(f"pA{i}", [128, 512], F32) for i in range(2)]
        pI = ps("pI", [128, 512], F32)
        pT = ps("pT", [128, 1024], BF16)
        for (t, s, k) in ((ident, io["ident"], "ident"), (onehot, io["onehot"], "onehot")):
            DMA(P, "sp", t[:], s[:, :], [], [k])
        for (t, s, k) in ((Cov, io["Cov"], "Cov"), (cmaskb, io["cmaskb"], "cmaskb"),
                          (lmaskb, io["lmaskb"], "lmaskb"), (selmul, io["selmul"], "selmul"), (seladd, io["seladd"], "seladd"),
                          (farmask, io["farmask"], "farmask")):
            DMA(P, "sp", t[:], s[:, :, :], [], [k])
        for i0 in range(0, 64, 16):
            DMA(P, "sp", E2[:, i0:i0 + 16, :], io["E2"][:, i0:i0 + 16, :], [], ["E2"])
        for i0 in range(0, 16, 4):
            DMA(P, "sp", wmaskb[:, i0:i0 + 4, :], io["wmaskb"][:, i0:i0 + 4, :], [], ["wmaskb"])
        DMA(P, "sp", gates[:], io["gates_s"].rearrange("(t p) g -> p t g", p=128), [], ["gates"])
        P.aop("pool", lambda e: e.memset(vbig[:, :, 128:129], 1.0), [], ["vbig"])
        P.aop("pool", lambda e: e.memset(vwe[:, :, 128:129], 1.0), [], ["vwe"])
        g3 = io["gathered"].rearrange("(r a) c -> r a c", r=NCORE)
        pv = payload_views(io["payload"], cfg)
        npt = [0]

        def attend(qT, qk, tiles, acc_first, acc_last, keep=None):
            kept = []
            n = len(tiles)

            def s_part(ti):
                kT_ap, kk, ml, mr, mk, v_ap, vk = tiles[ti]
                sbank = pS[npt[0] % 2]
                sk = f"pS{npt[0] % 2}"
                pt = pTs[npt[0] % 6]
                pk = f"pTs{npt[0] % 6}"
                npt[0] += 1
                MM(P, sbank[:], kT_ap, qT, True, False, kk + [qk], [sk])
                MM(P, sbank[:], ml, mr, False, True, mk, [sk])
                ACT(P, pt[:], sbank[:], AF.Exp, [sk], [pk], scale=SCALE)
                kept.append((pt, pk))

            def pv_part(ti):
                kT_ap, kk, ml, mr, mk, v_ap, vk = tiles[ti]
                pt, pk = kept[ti]
                for qs in range(4):
                    a = pA[qs // 2][:, (qs % 2) * 256:(qs % 2) * 256 + 129]
                    MM(P, a, pt[:, qs * 128:(qs + 1) * 128], v_ap, acc_first and ti == 0 and qs % 2 == 0,
                       acc_last and ti == n - 1, [pk] + vk, [f"pA{qs // 2}"])

            s_part(0)
            for ti in range(1, n):
                s_part(ti)
                pv_part(ti - 1)
            pv_part(n - 1)
            return kept

        def evac(b):
            for qs in range(4):
                a = pA[qs // 2][:, (qs % 2) * 256:(qs % 2) * 256 + 129]
                ACT(P, osb[:, b, qs, :], a, AF.Copy, [f"pA{qs // 2}"], ["osb"])
            TS(P, "dve", rden[:, b, :], osb[:, b, :, 128], 1e-30, None, ALU.add, None, ["osb"], ["rden"])
            P.aop("dve", lambda e: e.reciprocal(out=rden[:, b, :], in_=rden[:, b, :]), [], ["rden"])

        hq = 0
        for hk in range(4):
            for r in range(NCORE):
                DMA(P, "sp", kbig[:, r * TOK:(r + 1) * TOK], g3[r, hk * 128:(hk + 1) * 128, :], [], ["kbig"])
                vsrc = g3[r, 2048:2560, :].rearrange("a (b c) -> (a b) c", b=2)[:, hk * 128:(hk + 1) * 128]
                DMA(P, "sp", vbig[:, r * 8:(r + 1) * 8, 0:128], vsrc.rearrange("(t p) c -> p t c", p=128), [], ["vbig"])
                DMA(P, "sp", kwh[:, r, :], g3[r, 512 + hk * 128:512 + (hk + 1) * 128, 512:1024], [], ["kwh"])
                wsrc = g3[r, 2560:3072, :].rearrange("a (b c) -> (a b) c", b=2)[512:1024, hk * 128:(hk + 1) * 128]
                DMA(P, "sp", vwh[:, r, :, :], wsrc.rearrange("(t p) c -> p t c", p=128), [], ["vwh"])
            DMA(P, "sp", kwe[:, 512:1536], pv["kwT"][hk * 128:(hk + 1) * 128, :], [], ["kwe"])
            DMA(P, "sp", vwe[:, 4:12, 0:128], pv["vw"][:, hk * 128:(hk + 1) * 128].rearrange("(t p) c -> p t c", p=128),
                [], ["vwe"])
            TS(P, "dve", kwe[:, 0:512], kwh[:, 0, :], onehot[:, 0:1], None, ALU.mult, None, ["kwh", "onehot"], ["kwe"])
            TS(P, "dve", vwe[:, 0:4, 0:128], vwh[:, 0, :, :], onehot[:, 0:1], None, ALU.mult, None, ["vwh", "onehot"], ["vwe"])
            for r in range(1, 8):
                STT(P, "dve", kwe[:, 0:512], kwh[:, r, :], onehot[:, r:r + 1], kwe[:, 0:512], ALU.mult, ALU.add,
                    ["kwh", "onehot"], ["kwe"])
                STT(P, "dve", vwe[:, 0:4, 0:128], vwh[:, r, :, :], onehot[:, r:r + 1], vwe[:, 0:4, 0:128], ALU.mult, ALU.add,
                    ["vwh", "onehot"], ["vwe"])
            for c in range(2):
                c0 = c * CHK
                for g in range(4):
                    h = hk * 4 + g
                    qb = qTh[hq % 2]
                    qk = f"qTh{hq % 2}"
                    hq += 1
                    DMA(P, "sp", qb[:, 0:CHK], io["qT_s"][h, :, c0:c0 + CHK], [], [qk])
                    qT = qb[:, 0:CHK]
                    tiles = [(K.kcmpT[:, hk, jt * 128:(jt + 1) * 128], ["kcmpT"], ident[:], cmaskb[:, jt, c0:c0 + CHK],
                              ["ident", "cmaskb"], K.vcmp[:, hk, jt, :], ["vcmp"]) for jt in range(4)]
                    kept = attend(qT, qk, tiles, True, True)
                    evac(0)
                    for qs in range(4):
                        for jt in range(4):
                            pt, pk = kept[jt]
                            MM(P, pI[:, qs * 128:(qs + 1) * 128], pt[:, qs * 128:(qs + 1) * 128], Cov[:, jt, :], jt == 0, jt == 3,
                               [pk, "Cov"], ["pI"])
                    for qs in range(4):
                        if g == 0:
                            TS(P, "dve", impa[:, qs, :], pI[:, qs * 128:(qs + 1) * 128], rden[:, 0, qs:qs + 1], None, ALU.mult, None,
                               ["pI", "rden"], ["impa"])
                        else:
                            STT(P, "dve", impa[:, qs, :], pI[:, qs * 128:(qs + 1) * 128], rden[:, 0, qs:qs + 1], impa[:, qs, :],
                                ALU.mult, ALU.add, ["pI", "rden"], ["impa"])
                    for qs in range(4):
                        qt = c * 4 + qs
                        TT(P, "dve", gsc[:, 0, qs:qs + 1], rden[:, 0, qs:qs + 1], gates[:, qt, h * 3:h * 3 + 1], ALU.mult,
                           ["rden", "gates"], ["gsc"])
                        TS(P, "dve", K.ocmp[:, g, qs, :], osb[:, 0, qs, 0:128], gsc[:, 0, qs:qs + 1], None, ALU.mult, None,
                           ["osb", "gsc"], ["ocmp"])
                for qs in range(4):
                    qt = c * 4 + qs
                    TT(P, "dve", score[:], impa[:, qs, :], selmul[:, qt, :], ALU.mult, ["impa", "selmul"], ["score"])
                    TT(P, "dve", score[:], score[:], seladd[:, qt, :], ALU.add, ["seladd"], ["score"])
                    P.aop("dve", lambda e: e.max(out=m8a[:], in_=score[:]), ["score"], ["m8a"])
                    P.aop("dve", lambda e: e.match_replace(out=sc2[:], in_to_replace=m8a[:], in_values=score[:], imm_value=-3.0e38),
                          ["score", "m8a"], ["sc2"])
                    P.aop("dve", lambda e: e.max(out=m8b[:], in_=sc2[:]), ["sc2"], ["m8b"])
                    TS(P, "dve", sc2[:], score[:], m8b[:, 7:8], None, ALU.is_ge, None, ["score", "m8b"], ["sc2"])
                    TT(P, "dve", sc2[:], sc2[:], farmask[:, qt, :], ALU.mult, ["farmask"], ["sc2"])
                    TS(P, "dve", selb[:], sc2[:], -1.0, -NEGB, ALU.add, ALU.mult, ["sc2"], ["selb"])
                    TR(P, pT[:, 0:128], selb[:], ident[:], ["selb", "ident"], ["pT"])
                    ACT(P, selTb[:, qs * 128:(qs + 1) * 128], pT[:, 0:128], AF.Copy, ["pT"], ["selTb"])
                for g in range(4):
                    h = hk * 4 + g
                    qb = qTh[hq % 2]
                    qk = f"qTh{hq % 2}"
                    hq += 1
                    DMA(P, "sp", qb[:, 0:CHK], io["qT_s"][h, :, c0:c0 + CHK], [], [qk])
                    qT = qb[:, 0:CHK]
                    tiles = [(kbig[:, i * 128:(i + 1) * 128], ["kbig"], E2[:, i, :], selTb[:], ["E2", "selTb"], vbig[:, i, :], ["vbig"])
                             for i in range(64)]
                    attend(qT, qk, tiles, True, False)
                    tiles = [(K.ksown[:, hk, c0 + r * 128:c0 + (r + 1) * 128], ["ksown"], ident[:], lmaskb[:, r, :],
                              ["ident", "lmaskb"], K.vsown[:, hk, c * 4 + r, :], ["vsown"]) for r in range(4)]
                    attend(qT, qk, tiles, False, True)
                    evac(1)
                    tiles = [(kwe[:, (4 * c + r) * 128:(4 * c + r + 1) * 128], ["kwe"], ident[:], wmaskb[:, c * 8 + r, :],
                              ["ident", "wmaskb"], vwe[:, 4 * c + r, :], ["vwe"]) for r in range(8)]
                    attend(qT, qk, tiles, True, True)
                    evac(2)
                    for qs in range(4):
                        qt = c * 4 + qs
                        for b in (1, 2):
                            TT(P, "dve", gsc[:, b, qs:qs + 1], rden[:, b, qs:qs + 1], gates[:, qt, h * 3 + b:h * 3 + b + 1], ALU.mult,
                               ["rden", "gates"], ["gsc"])
                        STT(P, "dve", atm[:], osb[:, 1, qs, 0:128], gsc[:, 1, qs:qs + 1], K.ocmp[:, g, qs, :], ALU.mult, ALU.add,
                            ["osb", "gsc", "ocmp"], ["atm"])
                        STT(P, "dve", atb[:], osb[:, 2, qs, 0:128], gsc[:, 2, qs:qs + 1], atm[:], ALU.mult, ALU.add,
                            ["osb", "gsc", "atm"], ["atb"])
                        TR(P, pT[:, 128:256], atb[:], ident[:], ["atb", "ident"], ["pT"])
                        ACT(P, aTs[:, qs * 128:(qs + 1) * 128], pT[:, 128:256], AF.Copy, ["pT"], ["aTs"])
                    DMA(P, "sp", io["attnT_s"][h, :, c0:c0 + CHK], aTs[:], ["aTs"], [], name="st")
        P.barrier()
        P.flush()


def mixer_merge(P, nc, es, cfg, io):
    D, CH = cfg["D"], cfg["CH"]
    KT, CT = D // 128, CH // 128
    off = offsets(cfg)
    with ExitStack() as es2:
        C = setup_common(nc, P, es2, cfg, "m3", KT, nxs=1, with_sg=False, wslot=max(cfg["KG"] * 256, KT * 128, 2048, CT * 128))
        sb = C.sb
        aT = sb("aT", [128, 16, CHK], BF16)
        cT = sb("cT", [128, CT, CHK], BF16)
        sa = sb("sa", [128, CHK], F32)
        sbb = sb("sbb", [128, CHK], F32)
        t1 = sb("t1", [128, CHK], F32)
        consts_tok = load_consts(P, C, io["ident"], io["gcols"])
        Wv = io["w_in"].rearrange("(kt p) n -> p kt n", p=128)
        Wnv = io["nsa_w_o"].rearrange("(kt p) n -> p kt n", p=128)
        Wcv = io["conv_w_o"].rearrange("(kt p) n -> p kt n", p=128)
        Wov = io["w_out"].rearrange("(kt p) n -> p kt n", p=128)
        store_toks = {}
        for c in range(2):
            r0 = c * CHK
            P.barrier()
            C.hT_free = None
            C.ps_free = [None] * 6
            C.wreg_free = None
            for s in C.wd + C.xres + C.ost + C.xs:
                s.free = []
            t_hT = norm_transpose(P, C, io["x1"], r0, 1, consts_tok)
            P.lastw["hT"] = t_hT
            P.barrier()
            for h in range(16):
                DMA(P, "sp", aT[:, h, :], io["attnT_s"][h, :, r0:r0 + CHK], [], ["aT"])
            for ct in range(CT):
                DMA(P, "sp", cT[:, ct, :], io["convT_s"][ct * 128:(ct + 1) * 128, r0:r0 + CHK], [], ["cT"])
            for f in range(KT):
                wa, wak = load_w256(P, C, Wv, off["ga"] + f * 128, 128, None)
                wb, wbk = load_w256(P, C, Wv, off["gb"] + f * 128, 128, None)
                i = C.wu_i % 4
                C.wu_i += 1
                wnk = f"wslot{i}"
                wn = C.wreg[:, i * C.wslot:i * C.wslot + 16 * 128].rearrange("p (k n) -> p k n", n=128)
                DMA(P, "pool", wn, Wnv[:, :, f * 128:(f + 1) * 128], [], [wnk], name="w")
                i = C.wu_i % 4
                C.wu_i += 1
                wck = f"wslot{i}"
                wc = C.wreg[:, i * C.wslot:i * C.wslot + CT * 128].rearrange("p (k n) -> p k n", n=128)
                DMA(P, "pool", wc, Wcv[:, :, f * 128:(f + 1) * 128], [], [wck], name="w")
                proj_fm(P, C, wa, wak, 0, C.ps[0], "ps0")
                proj_fm(P, C, wb, wbk, 0, C.ps[1], "ps1")
                for kt in range(16):
                    MM(P, C.ps[2][:], wn[:, kt, :], aT[:, kt, :], kt == 0, kt == 15, [wnk, "aT"], ["ps2"])
                for kt in range(CT):
                    MM(P, C.ps[3][:], wc[:, kt, :], cT[:, kt, :], kt == 0, kt == CT - 1, [wck, "cT"], ["ps3"])
                ACT(P, sa[:], C.ps[0][:], AF.Sigmoid, ["ps0"], ["sa"])
                ACT(P, sbb[:], C.ps[1][:], AF.Sigmoid, ["ps1"], ["sbb"])
                TT(P, "dve", t1[:], sa[:], C.ps[2][:], ALU.mult, ["sa", "ps2"], ["t1"])
                TT(P, "dve", sbb[:], sbb[:], C.ps[3][:], ALU.mult, ["ps3"], ["sbb"])
                TT(P, "dve", C.act[:, f, :], t1[:], sbb[:], ALU.add, ["t1", "sbb"], ["mT"])
            P.barrier()
            down_proj(P, C, C.act, 0, KT, Wov, 1.0, io["x1"], io["x2"], r0, None, None, store_toks)
        P.barrier()
        P.flush()


def declare_io(nc, cfg, part):
    D, FF, CH = cfg["D"], cfg["FF"], cfg["CH"]
    KT, CT = D // 128, CH // 128
    off = offsets(cfg)
    R = payload_rows(cfg)
    io = {}

    def inp(name, shape, dt):
        io[name] = nc.dram_tensor(name, shape, dt, kind="ExternalInput").ap()

    def mid(name, shape, dt, producer):
        if part == "all":
            io[name] = nc.dram_tensor(name, shape, dt).ap()
        elif part == producer:
            io[name] = nc.dram_tensor(name, shape, dt, kind="ExternalOutput").ap()
        elif part == "B" and producer == "A":
            io[name] = nc.dram_tensor(name, shape, dt, kind="ExternalInput").ap()

    inp("ident", [128, 128], BF16)
    inp("gcols", [128, 3, KT], F32)
    inp("qkg", [128, 4], F32)
    if part in ("all", "A"):
        inp("x", [TOK, D], F32)
        inp("wg1", [D, FF], F32)
        inp("wu1", [D, FF], F32)
        inp("wd1", [FF, D], F32)
    inp("w_in", [D, off["nin"]], F32)
    mid("x1", [TOK, D], F32, "A")
    mid("payload", [R, 1024], BF16, "A")
    mid("qT_s", [16, 128, TOK], BF16, "A")
    mid("gates_s", [TOK, 48], F32, "A")
    mid("u_s", [CH, TOK], BF16, "A")
    if part == "all":
        io["gathered"] = nc.dram_tensor("gathered", [NCORE * R, 1024], BF16).ap()
    elif part == "B":
        inp("gathered", [NCORE * R, 1024], BF16)
    if part in ("all", "B"):
        for nm, shp, dt in (("onehot", [128, 8], F32), ("convw", [128, CT, 31], F32), ("convp", [128, CT, 3], F32),
                            ("posT", [128, 2, 32], F32), ("E2", [128, 64, 128], BF16), ("Cov", [128, 4, 128], BF16),
                            ("cmaskb", [128, 4, TOK], BF16), ("wmaskb", [128, 16, CHK], BF16), ("lmaskb", [128, 4, CHK], BF16),
                            ("selmul", [128, 8, 128], F32), ("seladd", [128, 8, 128], F32), ("farmask", [128, 8, 128], F32),
                            ("cmp_k_w1", [4096, 512], F32), ("cmp_k_w2", [512, 128], F32),
                            ("cmp_v_w1", [4096, 512], F32), ("cmp_v_w2", [512, 128], F32),
                            ("nsa_w_o", [2048, D], F32), ("conv_w_o", [CH, D], F32), ("w_out", [D, D], F32),
                            ("wg2", [D, FF], F32), ("wu2", [D, FF], F32), ("wd2", [FF, D], F32)):
            inp(nm, shp, dt)
        dk = dict(kind="ExternalOutput") if DEBUG else {}
        io["convT_s"] = nc.dram_tensor("convT_s", [CH, TOK], BF16, **dk).ap()
        io["attnT_s"] = nc.dram_tensor("attnT_s", [16, 128, TOK], BF16, **dk).ap()
        io["x2"] = nc.dram_tensor("x2", [TOK, D], F32, **dk).ap()
        io["out"] = nc.dram_tensor("out", [TOK, D], F32, kind="ExternalOutput").ap()
    io["tmp"] = nc.dram_tensor("tmp", [TOK, D], F32).ap()
    return io


def ffn_phase(P, nc, cfg, io, tag, xsrc, xdst, gi, wg, wu, wd, final):
    with ExitStack() as es2:
        C = setup_common(nc, P, es2, cfg, tag, cfg["FHMAX"])
        t_c = load_consts(P, C, io["ident"], io["gcols"])
        store_toks = {}
        ffn(P, C, xsrc, io["tmp"], xdst, gi, wg, wu, wd, t_c, store_toks)
        fw = None
        if final:
            fw = {"sp": [("slot:" + s.name, s.cnt) for s in C.ost if s.cnt]}
        P.barrier()
        P.flush(fw)


def build(cfg, part):
    nc = bass.Bass("TRN2", target_bir_lowering=False)
    io = declare_io(nc, cfg, part)
    with ExitStack() as es:
        P = Prog(nc, es)
        if part in ("all", "A"):
            ffn_phase(P, nc, cfg, io, "f1", io["x"], io["x1"], 0, io["wg1"], io["wu1"], io["wd1"], False)
            mixer_proj(P, nc, es, cfg, io)
        if part == "all":
            s = P.slot("cc")
            s.name = "cc_" + s.name
            P.dma("pool", s, lambda e: e.collective_compute("AllGather", ALU.bypass, replica_groups=[list(range(NCORE))],
                                                           ins=[io["payload"][:, :]], outs=[io["gathered"][:, :]]))
            s.cnt -= 15
            P._waits("pool", [("slot:" + s.name, 1)])
            P.barrier()
            P.flush()
        if part in ("all", "B"):
            mixer_conv(P, nc, es, cfg, io)
            esK = ExitStack()
            K = Ctx()
            K.kcmpT = esK.enter_context(nc.sbuf_tensor("K_kcmpT", [128, 4, 512], BF16))
            K.vcmp = esK.enter_context(nc.sbuf_tensor("K_vcmp", [128, 4, 4, 129], BF16))
            K.ksown = esK.enter_context(nc.sbuf_tensor("K_ksown", [128, 4, TOK], BF16))
            K.vsown = esK.enter_context(nc.sbuf_tensor("K_vsown", [128, 4, 8, 129], BF16))
            K.ocmp = esK.enter_context(nc.sbuf_tensor("K_ocmp", [128, 4, 4, 128], F32))
            pv = payload_views(io["payload"], cfg)
            for hk in range(4):
                DMA(P, "sp", K.ksown[:, hk, :], pv["ksT"][hk * 128:(hk + 1) * 128, :], [], ["ksown"])
                DMA(P, "sp", K.vsown[:, hk, :, 0:128],
                    pv["vs"][:, hk * 128:(hk + 1) * 128].rearrange("(t p) c -> p t c", p=128), [], ["vsown"])
            P.aop("pool", lambda e: e.memset(K.vsown[:, :, :, 128:129], 1.0), [], ["vsown"])
            mixer_compress(P, nc, es, cfg, io, K)
            mixer_attn(P, nc, es, cfg, io, K)
            esK.close()
            mixer_merge(P, nc, es, cfg, io)
            ffn_phase(P, nc, cfg, io, "f2", io["x2"], io["out"], 2, io["wg2"], io["wu2"], io["wd2"], True)
    return nc


def host_constants(cfg, core):
    import ml_dtypes
    bf = ml_dtypes.bfloat16
    c = {}
    base = core * TOK
    t = base + np.arange(TOK)
    j = np.arange(512)
    cmp_end = j * 16 + 31
    valid = (cmp_end[:, None] <= t[None, :]) & (j[:, None] <= 510)
    cm = np.where(valid, 0.0, NEGB).astype(np.float32)
    c["cmaskb"] = np.ascontiguousarray(cm.reshape(4, 128, TOK).transpose(1, 0, 2)).astype(bf)
    wm = np.zeros((128, 16, CHK), np.float32)
    for ch in range(2):
        for r in range(8):
            kpos = base + ch * CHK - 512 + r * 128 + np.arange(128)
            qpos = base + ch * CHK + np.arange(CHK)
            ok = (kpos[:, None] >= 0) & (kpos[:, None] <= qpos[None, :]) & (qpos[None, :] - kpos[:, None] < 512)
            wm[:, ch * 8 + r, :] = np.where(ok, 0.0, NEGB)
    c["wmaskb"] = wm.astype(bf)
    s = np.arange(128)
    cur = t // 64
    forced = (s[None, :] == 0) | (s[None, :] == cur[:, None]) | (s[None, :] == cur[:, None] - 1)
    validb = (s[None, :] * 64) <= t[:, None]
    selmul = (validb & ~forced).astype(np.float32)
    seladd = np.where(forced, 1e9, np.where(validb, 0.0, -1e30)).astype(np.float32)
    far = (s[None, :] < cur[:, None]).astype(np.float32)
    for nm, a in (("selmul", selmul), ("seladd", seladd), ("farmask", far)):
        c[nm] = np.ascontiguousarray(a.reshape(8, 128, 128).transpose(1, 0, 2))
    oh = np.zeros((128, 8), np.float32)
    if core > 0:
        oh[:, core - 1] = 1.0
    c["onehot"] = oh
    return c


def host_shared(cfg, inp):
    import ml_dtypes
    bf = ml_dtypes.bfloat16
    D, CH = cfg["D"], cfg["CH"]
    CT = CH // 128
    sh = {}
    sh["ident"] = np.eye(128, dtype=np.float32).astype(bf)
    g = np.stack([inp["ffn1_norm"][0], inp["mix_norm"][0], inp["ffn2_norm"][0]]).astype(np.float32)
    sh["gcols"] = np.ascontiguousarray(g.reshape(3, D // 128, 128).transpose(2, 0, 1))
    kn = inp["k_norm"][0]
    sh["qkg"] = np.ascontiguousarray(np.stack([inp["q_norm"][0], kn[0], kn[1], kn[2]], axis=1).astype(np.float32))
    sh["convw"] = np.ascontiguousarray(inp["conv_w"][0].T.reshape(CT, 128, 31).transpose(1, 0, 2).astype(np.float32))
    cp = np.stack([inp["conv_b"][0], inp["conv_ln_g"][0], inp["conv_ln_b"][0]], axis=1)
    sh["convp"] = np.ascontiguousarray(cp.reshape(CT, 128, 3).transpose(1, 0, 2).astype(np.float32))
    sh["posT"] = np.ascontiguousarray(np.stack([inp["cmp_pos_k"][0].T, inp["cmp_pos_v"][0].T], axis=1).astype(np.float32))
    E2 = np.zeros((128, 64, 128), np.float32)
    for i in range(64):
        E2[2 * i, i, 0:64] = 1.0
        E2[2 * i + 1, i, 64:128] = 1.0
    sh["E2"] = E2.astype(bf)
    cj = np.arange(512)[:, None] * 16
    si = np.arange(128)[None, :] * 64
    ov = np.clip(np.minimum(cj + 32, si + 64) - np.maximum(cj, si), 0, None) / 32.0
    ov[511, :] = 0.0
    sh["Cov"] = np.ascontiguousarray(ov.reshape(4, 128, 128).transpose(1, 0, 2)).astype(bf)
    lm = np.zeros((128, 4, CHK), np.float32)
    for r in range(4):
        kk = r * 128 + np.arange(128)
        q = np.arange(CHK)
        ok = (kk[:, None] // 64 == q[None, :] // 64) & (kk[:, None] <= q[None, :])
        lm[:, r, :] = np.where(ok, 0.0, NEGB)
    sh["lmaskb"] = lm.astype(bf)
    return sh


def kernel(**inputs):
    return run_kernel(CFG_FULL, inputs, MODE)


def run_kernel(cfg, inputs, mode):
    inp = {k: np.asarray(v) for k, v in inputs.items()}
    D = cfg["D"]
    xs = inp["x"].astype(np.float32).reshape(NCORE, TOK, D)
    sh = host_shared(cfg, inp)
    per = [host_constants(cfg, c) for c in range(NCORE)]
    wA = dict(wg1=inp["ffn1_w_gate"][0], wu1=inp["ffn1_w_up"][0], wd1=inp["ffn1_w_down"][0], w_in=inp["w_in"][0])
    wB = dict(w_in=inp["w_in"][0], cmp_k_w1=inp["cmp_k_w1"][0], cmp_k_w2=inp["cmp_k_w2"][0],
              cmp_v_w1=inp["cmp_v_w1"][0], cmp_v_w2=inp["cmp_v_w2"][0], nsa_w_o=inp["nsa_w_o"][0],
              conv_w_o=inp["conv_w_o"][0], w_out=inp["w_out"][0],
              wg2=inp["ffn2_w_gate"][0], wu2=inp["ffn2_w_up"][0], wd2=inp["ffn2_w_down"][0])
    shA = {k: sh[k] for k in ("ident", "gcols", "qkg")}
    shB = dict(sh)
    if mode == "fused":
        nc = build(cfg, "all")
        ims = [dict(shB, **dict(wA, **wB), **per[c], x=np.ascontiguousarray(xs[c])) for c in range(NCORE)]
        res = run_bass_kernel_spmd(nc, ims, core_ids=list(range(NCORE)))
        o = np.concatenate([r["out"] for r in res.results], axis=0)
        return o.reshape(1, SEQ, D).astype(np.float32)
    ncA = build(cfg, "A")
    imsA = [dict(shA, **wA, x=np.ascontiguousarray(xs[c])) for c in range(NCORE)]
    resA = run_bass_kernel_spmd(ncA, imsA, core_ids=list(range(NCORE)))
    gathered = np.concatenate([r["payload"] for r in resA.results], axis=0)
    ncB = build(cfg, "B")
    imsB = []
    for c in range(NCORE):
        r = resA.results[c]
        d = dict(shB, **wB, **per[c], gathered=gathered)
        for k in ("x1", "payload", "qT_s", "gates_s", "u_s"):
            d[k] = r[k]
        imsB.append(d)
    resB = run_bass_kernel_spmd(ncB, imsB, core_ids=list(range(NCORE)))
    if DEBUG:
        DBG["A"] = resA.results
        DBG["B"] = resB.results
    o = np.concatenate([r["out"] for r in resB.results], axis=0)
    return o.reshape(1, SEQ, D).astype(np.float32)
```
